# Optimizing a Trainium2 kernel written in Bass

```python
import jax, jax.numpy as jnp
from jax import lax
import numpy as np

D_MODEL = 1024
BATCH = 2
SEQ = 8192
DEPTH = 2

N_MIXERS = 2
N_POOL_GROUPS = 4
POOL_WINDOWS = (2, 4, 8, 16)
POOL_GROUP_DIM = D_MODEL // N_POOL_GROUPS
CONV_WIDTH = 31
D_FF = 4 * D_MODEL
ALPHA = (2.0 * DEPTH) ** 0.25
BETA = (8.0 * DEPTH) ** -0.25
LN_EPS = 1e-5
N_POOL_LAYERS = (DEPTH + 1) // 2
N_CONV_LAYERS = DEPTH // 2

kernel_name = "hybrid_pool_conformer_sqrelu_deepnorm"


def layer_norm(x, g, b):
    xf = x.astype(jnp.float32)
    mu = jnp.mean(xf, axis=-1, keepdims=True)
    var = jnp.mean(jnp.square(xf - mu), axis=-1, keepdims=True)
    y = (xf - mu) * lax.rsqrt(var + LN_EPS)
    return (y * g.astype(jnp.float32) + b.astype(jnp.float32)).astype(x.dtype)


def pool_mixer(x, pool_w, pool_scale):
    B, S, _ = x.shape
    t = jnp.arange(S, dtype=jnp.float32)[None, :, None]
    outs = []
    for g, w in enumerate(POOL_WINDOWS):
        xg = x[..., g * POOL_GROUP_DIM:(g + 1) * POOL_GROUP_DIM]
        c = jnp.cumsum(xg.astype(jnp.float32), axis=1)
        c_pad = jnp.concatenate([jnp.zeros((B, 1, POOL_GROUP_DIM), jnp.float32), c], axis=1)
        hi = c_pad[:, 1:]
        lo = jnp.pad(c_pad[:, :S + 1 - w], ((0, 0), (w - 1, 0), (0, 0)))
        count = jnp.minimum(t + 1.0, float(w))
        d = ((hi - lo) / count).astype(x.dtype) - xg
        outs.append(jnp.einsum('bsc,cd->bsd', d, pool_w[g]))
    return jnp.concatenate(outs, axis=-1) * pool_scale


def conv_module(x, w_in, b_in, dw, dw_b, ln_g, ln_b, w_out, b_out):
    h = jnp.einsum('bsd,de->bse', x, w_in) + b_in
    a, gate = jnp.split(h, 2, axis=-1)
    h = a * jax.nn.sigmoid(gate)
    h = lax.conv_general_dilated(
        h, dw.reshape(CONV_WIDTH, 1, D_MODEL).astype(h.dtype),
        window_strides=(1,), padding=[(CONV_WIDTH - 1, 0)],
        dimension_numbers=('NWC', 'WIO', 'NWC'),
        feature_group_count=D_MODEL) + dw_b
    h = layer_norm(h, ln_g, ln_b)
    h = jax.nn.silu(h)
    return jnp.einsum('bsd,de->bse', h, w_out) + b_out


def sqrelu_mlp(x, w1, b1, w2, b2):
    h = jnp.einsum('bsd,df->bsf', x, w1) + b1
    h = jnp.square(jax.nn.relu(h))
    return jnp.einsum('bsf,fd->bsd', h, w2) + b2


def setup_inputs(seed: int = 0) -> dict:
    key = jax.random.key(seed)
    ks = jax.random.split(key, 24)
    D, F, G, Dg, K = D_MODEL, D_FF, N_POOL_GROUPS, POOL_GROUP_DIM, CONV_WIDTH
    nrm = jax.random.normal
    P, C, L = N_POOL_LAYERS, N_CONV_LAYERS, DEPTH
    return {
        "x": nrm(ks[0], (BATCH, SEQ, D), jnp.float32),
        "pool_w": nrm(ks[1], (P, G, Dg, Dg), jnp.float32) * (Dg ** -0.5) * BETA,
        "pool_scale": 1.0 + 0.1 * nrm(ks[2], (P, D), jnp.float32),
        "conv_w_in": nrm(ks[3], (C, D, 2 * D), jnp.float32) * (D ** -0.5),
        "conv_b_in": 0.02 * nrm(ks[4], (C, 2 * D), jnp.float32),
        "conv_dw": nrm(ks[5], (C, K, D), jnp.float32) * (K ** -0.5),
        "conv_dw_b": 0.02 * nrm(ks[6], (C, D), jnp.float32),
        "conv_ln_g": 1.0 + 0.05 * nrm(ks[7], (C, D), jnp.float32),
        "conv_ln_b": 0.02 * nrm(ks[8], (C, D), jnp.float32),
        "conv_w_out": nrm(ks[9], (C, D, D), jnp.float32) * (D ** -0.5) * BETA,
        "conv_b_out": 0.02 * nrm(ks[10], (C, D), jnp.float32),
        "mix_ln_g": 1.0 + 0.05 * nrm(ks[11], (L, D), jnp.float32),
        "mix_ln_b": 0.02 * nrm(ks[12], (L, D), jnp.float32),
        "mlp_w1": nrm(ks[13], (L, D, F), jnp.float32) * (D ** -0.5) * BETA,
        "mlp_b1": 0.02 * nrm(ks[14], (L, F), jnp.float32),
        "mlp_w2": nrm(ks[15], (L, F, D), jnp.float32) * (F ** -0.5) * BETA,
        "mlp_b2": 0.02 * nrm(ks[16], (L, D), jnp.float32),
        "mlp_ln_g": 1.0 + 0.05 * nrm(ks[17], (L, D), jnp.float32),
        "mlp_ln_b": 0.02 * nrm(ks[18], (L, D), jnp.float32),
    }


def reference(x, pool_w, pool_scale, conv_w_in, conv_b_in, conv_dw, conv_dw_b,
              conv_ln_g, conv_ln_b, conv_w_out, conv_b_out, mix_ln_g, mix_ln_b,
              mlp_w1, mlp_b1, mlp_w2, mlp_b2, mlp_ln_g, mlp_ln_b):
    for i in range(DEPTH):
        j = i // N_MIXERS
        if i % N_MIXERS == 0:
            mix = pool_mixer(x, pool_w[j], pool_scale[j])
        else:
            mix = conv_module(x, conv_w_in[j], conv_b_in[j], conv_dw[j], conv_dw_b[j],
                              conv_ln_g[j], conv_ln_b[j], conv_w_out[j], conv_b_out[j])
        x = layer_norm(ALPHA * x + mix, mix_ln_g[i], mix_ln_b[i])
        x = layer_norm(ALPHA * x + sqrelu_mlp(x, mlp_w1[i], mlp_b1[i], mlp_w2[i], mlp_b2[i]),
                       mlp_ln_g[i], mlp_ln_b[i])
    return x
```

```python
import numpy as np
import concourse.bass as bass
import concourse.mybir as mybir
from concourse.bass_utils import run_bass_kernel_spmd

F32 = mybir.dt.float32
BF16 = mybir.dt.bfloat16
AF = mybir.ActivationFunctionType
ALU = mybir.AluOpType

NCORES = 8
D = 1024
NCH = 8
DFF = 4096
SEQ = 8192
BATCH = 2
TPC = 2048
HALO = 64
TT = HALO + TPC
KW = 31
ALPHA = 4.0 ** 0.25
EPS = 1e-5
POOLW = (2, 4, 8, 16)
RSLOTS = 6
XRES_ENG = "dve"
SLOT_ELEMS = 4096

HTILE = (0, 32, 32)
MTILES = [(1 + i, HALO + 512 * i, 512) for i in range(4)]
ALLTILES = [HTILE] + MTILES
PIPETILES = MTILES + [HTILE]

VEC_LAYOUT = {}
_off = 0


def _reg(name, n):
    global _off
    VEC_LAYOUT[name] = (_off, n)
    _off += n


_reg("pool_scale", 8)
for _l in range(2):
    _reg(f"mix_ln_g{_l}", 8)
    _reg(f"mix_ln_b{_l}", 8)
    _reg(f"mlp_b1_{_l}", 32)
    _reg(f"mlp_b2_{_l}", 8)
    _reg(f"mlp_ln_g{_l}", 8)
    _reg(f"mlp_ln_b{_l}", 8)
_reg("conv_b_in", 16)
_reg("conv_dw", 8 * KW)
_reg("conv_dw_b", 8)
_reg("conv_ln_g", 8)
_reg("conv_ln_b", 8)
_reg("conv_b_out", 8)
_reg("ident", 128)
NV = _off
NPC = 1 + 64


class Buf:
    __slots__ = ("name", "lw", "rd")

    def __init__(self, name):
        self.name = name
        self.lw = None
        self.rd = {}


class Op:
    __slots__ = ("eng", "fn", "reads", "writes", "sem", "ndma", "pe_acc", "idx", "waits", "sig")

    def __init__(self, eng, fn, reads, writes, sem=None, ndma=0, pe_acc=False):
        self.eng = eng
        self.fn = fn
        self.reads = reads
        self.writes = writes
        self.sem = sem
        self.ndma = ndma
        self.pe_acc = pe_acc
        self.idx = None
        self.waits = None
        self.sig = None


COMPUTE = ("pe", "act", "dve", "pool")
QUEUES = ("sp",)


class Prog:
    def __init__(self):
        self.ops = []

    def op(self, eng, fn, reads=(), writes=(), pe_acc=False):
        self.ops.append(Op(eng, fn, list(reads), list(writes), pe_acc=pe_acc))

    def dma(self, queue, sem, fn, reads=(), writes=(), ndma=1):
        self.ops.append(Op(queue, fn, list(reads), list(writes), sem=sem, ndma=ndma))

    def resolve(self):
        known = {}
        cnt = {}
        snaps = {}
        needed = set()
        for op in self.ops:
            E = op.eng
            S = op.sem if op.sem is not None else E
            kn = known.setdefault(E, {})
            deps = set()
            for b in op.reads:
                if b.lw is not None:
                    deps.add(b.lw)
            for b in op.writes:
                if b.lw is not None:
                    if not (op.pe_acc and b.lw[0] == "pe"):
                        deps.add(b.lw)
                for it in b.rd.items():
                    deps.add(it)
            waits = []
            for (De, Di) in sorted(deps, key=lambda t: -t[1]):
                if kn.get(De, -1) >= Di:
                    continue
                waits.append((De, Di))
                needed.add((De, Di))
                kn[De] = Di
                for k2, v2 in snaps[(De, Di)].items():
                    if kn.get(k2, -1) < v2:
                        kn[k2] = v2
            idx = cnt.get(S, 0)
            cnt[S] = idx + 1
            op.idx = idx
            op.waits = waits
            op.sig = S
            snaps[(S, idx)] = dict(kn)
            for b in op.reads:
                b.rd[S] = idx
            for b in op.writes:
                b.lw = (S, idx)
                b.rd = {}
        self.needed = needed
        self.val = {}
        run = {}
        for op in self.ops:
            S = op.sig
            if op.sem is not None:
                run[S] = run.get(S, 0) + 16 * op.ndma
                self.val[(S, op.idx)] = run[S]
            else:
                if (S, op.idx) in needed:
                    run[S] = run.get(S, 0) + 1
                    self.val[(S, op.idx)] = run[S]
        self.sem_names = sorted(cnt.keys())
        self.final = dict(run)

    def emit_engine(self, E, handle, sems):
        for op in self.ops:
            if op.eng != E:
                continue
            for (De, Di) in op.waits:
                handle.wait_ge(sems[De], self.val[(De, Di)])
            r = op.fn(handle)
            if op.sem is not None:
                insts = r if isinstance(r, (list, tuple)) else [r]
                assert len(insts) == op.ndma
                for i in insts:
                    i.then_inc(sems[op.sem], 16)
            elif (op.sig, op.idx) in self.needed:
                r.then_inc(sems[op.sig], 1)


def build_program(dbg_stage=None):
    nc = bass.Bass("TRN2", target_bir_lowering=False)
    P = Prog()

    xT = nc.dram_tensor("xT", [D, TT], F32, kind="ExternalInput").ap()
    vecs_d = nc.dram_tensor("vecs", [128, NV], F32, kind="ExternalInput").ap()
    pc_d = nc.dram_tensor("pcore", [128, NPC], F32, kind="ExternalInput").ap()
    pool_w_d = nc.dram_tensor("pool_w", [1024, 256], F32, kind="ExternalInput").ap()
    cwin_d = nc.dram_tensor("conv_w_in", [1024, 2048], F32, kind="ExternalInput").ap()
    cwout_d = nc.dram_tensor("conv_w_out", [1024, 1024], F32, kind="ExternalInput").ap()
    w1_d = nc.dram_tensor("mlp_w1", [2 * 1024, DFF], F32, kind="ExternalInput").ap()
    w2_d = nc.dram_tensor("mlp_w2", [2 * DFF, 1024], F32, kind="ExternalInput").ap()
    yT = nc.dram_tensor("yT", [4 * NCH * 128, 512], F32, kind="ExternalOutput").ap()
    dbg_d = None
    if dbg_stage is not None:
        dbg_d = nc.dram_tensor("dbg", [D, TT], F32, kind="ExternalOutput").ap()

    A = nc.alloc_sbuf_tensor
    XB = A("XB", [128, NCH, TT], F32).ap()
    XBF = A("XBF", [128, NCH, TT], BF16).ap()
    HB = A("HB", [128, NCH, TT], BF16).ap()
    RING = A("RING", [128, RSLOTS, SLOT_ELEMS], BF16).ap()
    VEC = A("VEC", [128, NV], F32).ap()
    PCV = A("PCV", [128, NPC], F32).ap()
    DV = A("DV", [128, 64], F32).ap()
    ONES = A("ONES", [128, 128], BF16).ap()
    ONES32 = A("ONES32", [128, 128], F32).ap()
    SG = A("SG", [128, 2, 512], F32).ap()
    VBT = A("VBT", [128, 3, 512], BF16).ap()
    VSQ = A("VSQ", [128, 3, 512], BF16).ap()
    RT = A("RT", [128, 4, 512], BF16).ap()
    LNA = A("LNA", [128, 2, 512], F32).ap()
    LNB = A("LNB", [128, 2, 512], F32).ap()
    PS = [nc.alloc_psum_tensor(f"PS{i}", [128, 512], F32).ap() for i in range(8)] \
        if hasattr(nc, "alloc_psum_tensor") else None
    assert PS is not None

    bXB = [[Buf(f"XB{c}_{t}") for t in range(5)] for c in range(NCH)]
    bXBF = [[Buf(f"XBF{c}_{t}") for t in range(5)] for c in range(NCH)]
    bHB = [[Buf(f"HB{c}_{t}") for t in range(5)] for c in range(NCH)]
    bRING = [Buf(f"RING{s}") for s in range(RSLOTS)]
    bVEC, bPCV, bDV, bONES = Buf("VEC"), Buf("PCV"), Buf("DV"), Buf("ONES")
    bSG = [Buf("SG0"), Buf("SG1")]
    bVBT = [Buf(f"VBT{i}") for i in range(3)]
    bVSQ = [Buf(f"VSQ{i}") for i in range(3)]
    bRT = [Buf(f"RT{i}") for i in range(4)]
    bLNA = [Buf("LNA0"), Buf("LNA1")]
    bLNB = [Buf("LNB0"), Buf("LNB1")]
    bPS = [Buf(f"PS{i}") for i in range(8)]
    bOUT = Buf("OUT")

    def vcol(name, j=0, n=1):
        o, _ = VEC_LAYOUT[name]
        return VEC[:, o + j:o + j + n]

    def dvcol(j, n=1):
        return DV[:, j:j + n]

    DV_AG = {}
    _dvo = [0]

    def dv_alloc(name, n=8):
        DV_AG[name] = _dvo[0]
        _dvo[0] += n
        return DV_AG[name]

    free_slots = list(range(RSLOTS))
    pending = []
    granted = {}

    def request(name, loader):
        pending.append((name, loader))
        pump()

    def pump():
        while pending and free_slots:
            name, loader = pending.pop(0)
            s = free_slots.pop(0)
            granted[name] = s
            if loader is not None:
                loader(s)

    def release(name):
        s = granted.pop(name)
        free_slots.append(s)
        pump()

    def slot_of(name):
        assert name in granted, f"unit {name} not granted (ring too small?)"
        return granted[name]

    ring_dep = []

    def load_cols(src2d, row0, col0, ncols):
        def loader(s):
            dst = RING[:, s, 0:8 * ncols].rearrange("p (k n) -> p k n", k=8)
            src = src2d[row0:row0 + 1024, col0:col0 + ncols].rearrange("(k p) n -> p k n", p=128)
            P.dma("pool", f"ring{s}", lambda g, dst=dst, src=src: g.dma_start(out=dst, in_=src),
                  reads=list(ring_dep), writes=[bRING[s]])
        return loader

    def load_rows(src2d, row0, nk, ncols):
        def loader(s):
            dst = RING[:, s, 0:nk * ncols].rearrange("p (k n) -> p k n", k=nk)
            src = src2d[row0:row0 + nk * 128, 0:ncols].rearrange("(k p) n -> p k n", p=128)
            P.dma("pool", f"ring{s}", lambda g, dst=dst, src=src: g.dma_start(out=dst, in_=src),
                  reads=list(ring_dep), writes=[bRING[s]])
        return loader

    def load_glu(u):
        def loader(s):
            dst = RING[:, s, :].rearrange("p (k n) -> p k n", k=8)
            sa = cwin_d[:, 256 * u:256 * u + 256].rearrange("(k p) n -> p k n", p=128)
            sg = cwin_d[:, 1024 + 256 * u:1024 + 256 * u + 256].rearrange("(k p) n -> p k n", p=128)
            P.dma("pool", f"ring{s}",
                  lambda g: [g.dma_start(out=dst[:, :, 0:256], in_=sa),
                             g.dma_start(out=dst[:, :, 256:512], in_=sg)],
                  reads=list(ring_dep), writes=[bRING[s]], ndma=2)
        return loader

    P.dma("sp", "vec", lambda e: e.dma_start(out=VEC, in_=vecs_d), writes=[bVEC])
    P.dma("sp", "pcv", lambda e: e.dma_start(out=PCV, in_=pc_d), writes=[bPCV])
    for c in range(NCH):
        P.dma("sp", f"xin{c}",
              lambda e, c=c: e.dma_start(out=XB[:, c, :], in_=xT[c * 128:(c + 1) * 128, :]),
              writes=bXB[c])
    P.op("dve", lambda e: e.memset(ONES, 1.0 / 1024.0), writes=[bONES])
    bONES32 = Buf("ONES32")
    P.op("dve", lambda e: e.memset(ONES32, 1.0 / 1024.0), writes=[bONES32])

    def pool_loader(s):
        dst = RING[:, s, 0:2048].rearrange("p (g k n) -> p g k n", g=4, k=2)
        src = pool_w_d.rearrange("(g k p) n -> p g k n", g=4, k=2)
        P.dma("pool", f"ring{s}", lambda g: g.dma_start(out=dst, in_=src), writes=[bRING[s]])

    def queue_mlp_units(l):
        for q in range(4):
            for h in range(2):
                request(f"w1_{l}_{2 * q + h}", load_cols(w1_d, l * 1024, (2 * q + h) * 512, 512))
            for h in range(2):
                request(f"w2_{l}_{2 * q + h}", load_rows(w2_d, l * DFF + (2 * q + h) * 512, 4, 1024))

    def derive(gname, bname, nextbias):
        og = dv_alloc(gname + "_ag")
        ob = dv_alloc(bname + "_ab")
        P.op("dve", lambda e: e.tensor_scalar(out=dvcol(og, 8), in0=vcol(gname, 0, 8), scalar1=ALPHA,
                                              scalar2=None, op0=ALU.mult),
             reads=[bVEC], writes=[bDV])
        P.op("dve", lambda e: e.scalar_tensor_tensor(out=dvcol(ob, 8), in0=vcol(bname, 0, 8), scalar=ALPHA,
                                                     in1=vcol(nextbias, 0, 8), op0=ALU.mult, op1=ALU.add),
             reads=[bVEC], writes=[bDV])
        return og, ob

    AG = {}
    AG["mix0"] = derive("mix_ln_g0", "mix_ln_b0", "mlp_b2_0")
    AG["mlp0"] = derive("mlp_ln_g0", "mlp_ln_b0", "conv_b_out")
    AG["mix1"] = derive("mix_ln_g1", "mix_ln_b1", "mlp_b2_1")

    gemm_bank = [0]
    gemm_banks = [[0, 1, 2, 3, 6, 7]]

    def next_bank():
        lst = gemm_banks[0]
        gemm_bank[0] = (gemm_bank[0] + 1) % len(lst)
        return lst[gemm_bank[0]]

    tmp_rot = {"sg": 0, "vbt": 0, "vsq": 0, "rt": 0, "ln": 0}
    rt_ctr = [0]

    def rot(name):
        v = tmp_rot[name]
        tmp_rot[name] = 1 - v
        return v

    ln_set = {}

    def ln_s1(tile):
        (ti, off, n) = tile
        for c in range(NCH):
            src = XB[:, c, off:off + n]
            P.op("act", lambda e, c=c, src=src, off=off, n=n: e.activation(out=HB[:, c, off:off + n], in_=src, func=AF.Square),
                 reads=[bXB[c][ti]], writes=[bHB[c][ti]])
            P.op("dve", lambda e, c=c, src=src, off=off, n=n: e.tensor_copy(out=XBF[:, c, off:off + n], in_=src),
                 reads=[bXB[c][ti]], writes=[bXBF[c][ti]])
            yield

    def ln_s2(tile):
        (ti, off, n) = tile
        st = rot("ln")
        ln_set[ti] = st
        bm, be = 4, 5
        for c in range(NCH):
            P.op("pe", lambda e, n=n, bm=bm, c=c, off=off: e.matmul(PS[bm][:, 0:n], lhsT=ONES, rhs=XBF[:, c, off:off + n],
                                                                   start=(c == 0), stop=(c == NCH - 1)),
                 reads=[bONES, bXBF[c][ti]], writes=[bPS[bm]], pe_acc=(c > 0))
            P.op("pe", lambda e, n=n, be=be, c=c, off=off: e.matmul(PS[be][:, 0:n], lhsT=ONES, rhs=HB[:, c, off:off + n],
                                                                   start=(c == 0), stop=(c == NCH - 1)),
                 reads=[bONES, bHB[c][ti]], writes=[bPS[be]], pe_acc=(c > 0))
        b = LNB[:, st, 0:n]
        P.op("act", lambda e: e.activation(out=b, in_=PS[bm][:, 0:n], func=AF.Square),
             reads=[bPS[bm]], writes=[bLNB[st]])
        P.op("dve", lambda e: e.scalar_tensor_tensor(out=b, in0=PS[be][:, 0:n], scalar=EPS, in1=b,
                                                     op0=ALU.add, op1=ALU.subtract),
             reads=[bPS[be], bLNB[st]], writes=[bLNB[st]])
        P.op("act", lambda e: e.activation(out=b, in_=b, func=AF.Ln),
             reads=[bLNB[st]], writes=[bLNB[st]])
        P.op("act", lambda e: e.activation(out=b, in_=b, func=AF.Exp, scale=-0.5),
             reads=[bLNB[st]], writes=[bLNB[st]])

    def ln_s3(tile, gname, bname, mode, ag=None, xres_eng="dve"):
        (ti, off, n) = tile
        st = ln_set[ti]
        bm = 4
        b = LNB[:, st, 0:n]
        for c in range(NCH):
            dst = XB[:, c, off:off + n]
            P.op("dve", lambda e, dst=dst, bm=bm, n=n: e.tensor_tensor(out=dst, in0=dst, in1=PS[bm][:, 0:n], op=ALU.subtract),
                 reads=[bXB[c][ti], bPS[bm]], writes=[bXB[c][ti]])
            yield
        for c in range(NCH):
            dst = XB[:, c, off:off + n]
            P.op("dve", lambda e, dst=dst, st=st, n=n: e.tensor_tensor(out=dst, in0=dst, in1=LNB[:, st, 0:n], op=ALU.mult),
                 reads=[bXB[c][ti], bLNB[st]], writes=[bXB[c][ti]])
            if mode in ("mid", "midx"):
                P.op("act", lambda e, dst=dst, c=c, off=off, n=n: e.activation(
                    out=XBF[:, c, off:off + n], in_=dst, func=AF.Identity,
                    bias=vcol(bname, c), scale=vcol(gname, c)),
                    reads=[bXB[c][ti], bVEC], writes=[bXBF[c][ti]])
            yield
        if mode == "midx":
            return
        for c in range(NCH):
            dst = XB[:, c, off:off + n]
            if mode == "mid":
                og, ob = ag
                if xres_eng == "act":
                    P.op("act", lambda e, dst=dst, c=c, og=og, ob=ob: e.activation(
                        out=dst, in_=dst, func=AF.Identity, bias=dvcol(ob + c), scale=dvcol(og + c)),
                        reads=[bXB[c][ti], bDV], writes=[bXB[c][ti]])
                else:
                    P.op("dve", lambda e, dst=dst, c=c, og=og, ob=ob: e.tensor_scalar(
                        out=dst, in0=dst, scalar1=dvcol(og + c), scalar2=dvcol(ob + c), op0=ALU.mult, op1=ALU.add),
                        reads=[bXB[c][ti], bDV], writes=[bXB[c][ti]])
            else:
                P.op("act", lambda e, dst=dst, c=c: e.activation(
                    out=dst, in_=dst, func=AF.Identity, bias=vcol(bname, c), scale=vcol(gname, c)),
                    reads=[bXB[c][ti], bVEC], writes=[bXB[c][ti]])
                t0 = off - HALO
                if c % 2 == 1:
                    r0 = ((t0 // 512) * NCH + c - 1) * 128
                    P.dma("sp", "out", lambda e, c=c, r0=r0, off=off, n=n: e.dma_start(
                        out=yT[r0:r0 + 256, 0:n].rearrange("(c p) n -> p c n", p=128),
                        in_=XB[:, c - 1:c + 1, off:off + n]),
                        reads=[bXB[c - 1][ti], bXB[c][ti]], writes=[bOUT])

    def run(gen):
        if gen is not None:
            for _ in gen:
                pass

    def merge(a, b, ra=1, rb=1):
        a = iter(a) if a is not None else iter(())
        b = iter(b) if b is not None else iter(())
        da = db = False
        while not (da and db):
            for _ in range(ra):
                if not da:
                    try:
                        next(a)
                    except StopIteration:
                        da = True
            for _ in range(rb):
                if not db:
                    try:
                        next(b)
                    except StopIteration:
                        db = True

    def pipeline(tiles, X, S3, Y):
        T = len(tiles)
        for s_ in range(T + 2):
            gx = X(tiles[s_]) if (s_ < T and X is not None) else None
            g3 = S3(tiles[s_ - 1]) if 0 <= s_ - 1 < T else None
            merge(gx, g3, 1, 2)
            g1 = ln_s1(tiles[s_]) if s_ < T else None
            gy = Y(tiles[s_ - 2]) if (0 <= s_ - 2 < T and Y is not None) else None
            merge(g1, gy, 4, 1)
            if s_ < T:
                ln_s2(tiles[s_])

    def ln_apply_stats(st, bm, be, n):
        a = LNA[:, st, 0:n]
        b = LNB[:, st, 0:n]
        P.op("act", lambda e: e.activation(out=a, in_=PS[bm][:, 0:n], func=AF.Identity),
             reads=[bPS[bm]], writes=[bLNA[st]])
        P.op("act", lambda e: e.activation(out=b, in_=PS[bm][:, 0:n], func=AF.Square),
             reads=[bPS[bm]], writes=[bLNB[st]])
        P.op("dve", lambda e: e.scalar_tensor_tensor(out=b, in0=PS[be][:, 0:n], scalar=EPS, in1=b,
                                                     op0=ALU.add, op1=ALU.subtract),
             reads=[bPS[be], bLNB[st]], writes=[bLNB[st]])
        P.op("act", lambda e: e.activation(out=b, in_=b, func=AF.Ln),
             reads=[bLNB[st]], writes=[bLNB[st]])
        P.op("act", lambda e: e.activation(out=b, in_=b, func=AF.Exp, scale=-0.5),
             reads=[bLNB[st]], writes=[bLNB[st]])
        P.op("dve", lambda e: e.scalar_tensor_tensor(out=a, in0=a, scalar=-1.0, in1=b, op0=ALU.mult, op1=ALU.mult),
             reads=[bLNA[st], bLNB[st]], writes=[bLNA[st]])

    def gemm1_group(l, q, j, tile):
        (ti, off, n) = tile
        b1 = f"mlp_b1_{l}"
        uname = f"w1_{l}_{2 * q + j // 4}"
        s = slot_of(uname)
        W = RING[:, s, :].rearrange("p (k n) -> p k n", k=8)
        co = (j % 4) * 128
        J = 8 * q + j
        bk = next_bank()
        for k in range(8):
            P.op("pe", lambda e, bk=bk, n=n, W=W, k=k, co=co, off=off: e.matmul(
                PS[bk][:, 0:n], lhsT=W[:, k, co:co + 128], rhs=XBF[:, k, off:off + n],
                start=(k == 0), stop=(k == 7)),
                reads=[bRING[s], bXBF[k][ti]], writes=[bPS[bk]], pe_acc=(k > 0))
        rt_ctr[0] = (rt_ctr[0] + 1) % 4
        ri = rt_ctr[0]
        P.op("act", lambda e, ri=ri, n=n, bk=bk, J=J, b1=b1: e.activation(
            out=RT[:, ri, 0:n], in_=PS[bk][:, 0:n], func=AF.Relu, bias=vcol(b1, J)),
            reads=[bPS[bk], bVEC], writes=[bRT[ri]])
        P.op("dve", lambda e, ri=ri, n=n, j=j, off=off: e.tensor_tensor(
            out=HB[:, j, off:off + n], in0=RT[:, ri, 0:n], in1=RT[:, ri, 0:n], op=ALU.mult),
            reads=[bRT[ri]], writes=[bHB[j][ti]])

    def gemm2_tile(l, q, tile):
        (ti, off, n) = tile
        s0 = slot_of(f"w2_{l}_{2 * q}")
        s1 = slot_of(f"w2_{l}_{2 * q + 1}")
        Ws = [RING[:, s0, :].rearrange("p (k n) -> p k n", k=4),
              RING[:, s1, :].rearrange("p (k n) -> p k n", k=4)]
        bs = [bRING[s0], bRING[s1]]
        for m in range(8):
            bk = next_bank()
            for j in range(8):
                P.op("pe", lambda e, bk=bk, n=n, j=j, m=m, off=off, Ws=Ws: e.matmul(
                    PS[bk][:, 0:n], lhsT=Ws[j // 4][:, j % 4, m * 128:(m + 1) * 128],
                    rhs=HB[:, j, off:off + n], start=(j == 0), stop=(j == 7)),
                    reads=[bs[j // 4], bHB[j][ti]], writes=[bPS[bk]], pe_acc=(j > 0))
            dst = XB[:, m, off:off + n]
            og, ob = AG["mix0"] if l == 0 else AG["mix1"]
            if q == 0:
                P.op("dve", lambda e, dst=dst, bk=bk, n=n, m=m, og=og: e.scalar_tensor_tensor(
                    out=dst, in0=dst, scalar=dvcol(og + m), in1=PS[bk][:, 0:n], op0=ALU.mult, op1=ALU.add),
                    reads=[bPS[bk], bXB[m][ti], bDV], writes=[bXB[m][ti]])
            elif q == 1:
                P.op("dve", lambda e, dst=dst, bk=bk, n=n, m=m, ob=ob: e.scalar_tensor_tensor(
                    out=dst, in0=PS[bk][:, 0:n], scalar=dvcol(ob + m), in1=dst, op0=ALU.add, op1=ALU.add),
                    reads=[bPS[bk], bXB[m][ti], bDV], writes=[bXB[m][ti]])
            else:
                P.op("dve", lambda e, dst=dst, bk=bk, n=n: e.tensor_tensor(
                    out=dst, in0=PS[bk][:, 0:n], in1=dst, op=ALU.add),
                    reads=[bPS[bk], bXB[m][ti]], writes=[bXB[m][ti]])
            yield

    def mlp_head(l, tile):
        for j in range(8):
            gemm1_group(l, 0, j, tile)
            yield

    def mlp_rest(l, tiles):
        release(f"w1_{l}_0")
        release(f"w1_{l}_1")
        for q in range(4):
            if q > 0:
                for j in range(8):
                    for tile in tiles:
                        gemm1_group(l, q, j, tile)
                    if j % 4 == 3:
                        release(f"w1_{l}_{2 * q + j // 4}")
            if q < 3:
                for tile in tiles:
                    run(gemm2_tile(l, q, tile))
                release(f"w2_{l}_{2 * q}")
                release(f"w2_{l}_{2 * q + 1}")

    def mlp_tail_done(l):
        release(f"w2_{l}_6")
        release(f"w2_{l}_7")

    def dump(stage):
        if dbg_stage == stage:
            for c in range(NCH):
                P.dma("sp", "out", lambda e, c=c: e.dma_start(out=dbg_d[c * 128:(c + 1) * 128, :], in_=XB[:, c, :]),
                      reads=bXB[c], writes=[bOUT])
            return True
        return False

    request("pool_w", pool_loader)
    ring_dep.extend(bXB[7])
    queue_mlp_units(0)
    for u in range(4):
        request(f"cwin{u}", load_glu(u))
    for h in range(2):
        request(f"cwout{h}", load_cols(cwout_d, 0, 512 * h, 512))
    request("SB0", None)
    request("SB1", None)
    request("DG0", None)
    request("DG1", None)
    queue_mlp_units(1)
    del ring_dep[:]

    def finish():
        return nc, P

    DUM = A("DUM", [128, 8], F32).ap()

    bDUM = Buf("DUM")

    def fence(bufs):
        P.op("dve", lambda e: e.memset(DUM[:, 0:1], 0.0), reads=[], writes=list(bufs) + [bDUM])

    P.op("dve", lambda e: e.memset(DUM, 0.0), reads=[], writes=[bDUM])

    SCR = HB.rearrange("p c t -> p (c t)").bitcast(F32)
    scr = [SCR[:, i * TT:(i + 1) * TT] for i in range(4)]
    bS4 = [Buf(f"SCR{i}") for i in range(4)]
    allHB = [b for c in range(NCH) for b in bHB[c]]
    for g in range(4):
        w = POOLW[g]
        cs = (2 * g, 2 * g + 1)
        if g == 0:
            for c in cs:
                xrow = XB[:, c, :]
                P.op("dve", lambda e, c=c, xrow=xrow: e.tensor_tensor(
                    out=XBF[:, c, 32:TT], in0=xrow[:, 31:TT - 1], in1=xrow[:, 32:TT], op=ALU.subtract),
                    reads=bXB[c], writes=bXBF[c])
            for c in cs:
                P.op("dve", lambda e, c=c: e.tensor_scalar(
                    out=XBF[:, c, HALO:HALO + 1], in0=XBF[:, c, HALO:HALO + 1], scalar1=PCV[:, 0:1],
                    scalar2=None, op0=ALU.mult),
                    reads=[bXBF[c][1], bPCV], writes=[bXBF[c][1]])
            for c in cs:
                xrow = XB[:, c, :]
                P.op("act", lambda e, xrow=xrow: e.activation(out=xrow[:, 32:TT], in_=xrow[:, 32:TT], func=AF.Identity,
                                                              scale=ALPHA),
                     reads=bXB[c], writes=bXB[c])
            continue
        cur = {c: (XB[:, c, :], bXB[c]) for c in cs}
        lo = 0
        sh = 1
        if g == 3:
            for ci, c in enumerate(cs):
                xrow = XB[:, c, :]
                ea = scr[2 * ci]
                P.op("dve", lambda e, ea=ea, xrow=xrow: e.tensor_tensor(
                    out=ea[:, 16:TT], in0=xrow[:, 16:TT], in1=xrow[:, 0:TT - 16], op=ALU.subtract),
                    reads=bXB[c], writes=[bS4[2 * ci]])
            for ci, c in enumerate(cs):
                xrow = XB[:, c, :]
                ea = scr[2 * ci]
                P.op("dve", lambda e, ea=ea, xrow=xrow: e.tensor_copy(out=ea[:, 0:16], in_=xrow[:, 0:16]),
                     reads=bXB[c] + [bS4[2 * ci]], writes=[bS4[2 * ci]])
            for ci, c in enumerate(cs):
                ea = scr[2 * ci]
                sa = scr[2 * ci + 1]
                P.op("dve", lambda e, ea=ea, sa=sa: e.tensor_tensor_scan(
                    out=sa[:, 0:TT], data0=ea[:, 0:TT], data1=DUM[:, 0:1].to_broadcast([128, TT]),
                    initial=0.0, op0=ALU.add, op1=ALU.add),
                    reads=[bS4[2 * ci], bDUM], writes=[bS4[2 * ci + 1]])
                cur[c] = (sa, [bS4[2 * ci + 1]])
        for stp in range(g + 1 if g < 3 else 0):
            lo = lo + sh
            for ci, c in enumerate(cs):
                si_ = 2 * ci + (stp % 2)
                dstb = scr[si_]
                src, sb = cur[c]
                P.op("dve", lambda e, dstb=dstb, src=src, lo=lo, sh=sh: e.tensor_tensor(
                    out=dstb[:, lo:TT], in0=src[:, lo:TT], in1=src[:, lo - sh:TT - sh], op=ALU.add),
                    reads=sb, writes=[bS4[si_]])
                cur[c] = (dstb, [bS4[si_]])
            sh *= 2
        for ci, c in enumerate(cs):
            src, sb = cur[c]
            xrow = XB[:, c, :]
            P.op("dve", lambda e, src=src, c=c, w=w, xrow=xrow: e.scalar_tensor_tensor(
                out=XBF[:, c, 32:TT], in0=src[:, 32:TT], scalar=1.0 / w, in1=xrow[:, 32:TT],
                op0=ALU.mult, op1=ALU.subtract),
                reads=bXB[c] + sb, writes=bXBF[c])
        tmps = {}
        for ci, c in enumerate(cs):
            src, sb = cur[c]
            ti_ = 2 * ci + ((g + 1) % 2)
            tmpf = scr[ti_]
            tmps[c] = (tmpf, ti_)
            P.op("dve", lambda e, src=src, g=g, tmpf=tmpf: e.tensor_tensor(
                out=tmpf[:, 0:16], in0=src[:, HALO:HALO + 16], in1=PCV[:, 1 + 16 * g:1 + 16 * g + 16], op=ALU.mult),
                reads=sb + [bPCV], writes=[bS4[ti_]])
        for ci, c in enumerate(cs):
            tmpf, ti_ = tmps[c]
            xrow = XB[:, c, :]
            P.op("dve", lambda e, c=c, tmpf=tmpf, xrow=xrow: e.tensor_tensor(
                out=XBF[:, c, HALO:HALO + 16], in0=tmpf[:, 0:16], in1=xrow[:, HALO:HALO + 16], op=ALU.subtract),
                reads=[bS4[ti_]] + bXB[c], writes=[bXBF[c][0], bXBF[c][1]])
        for ci, c in enumerate(cs):
            xrow = XB[:, c, :]
            P.op("act", lambda e, xrow=xrow: e.activation(out=xrow[:, 32:TT], in_=xrow[:, 32:TT], func=AF.Identity,
                                                          scale=ALPHA),
                 reads=bXB[c], writes=bXB[c])
    fence(allHB + bS4)
    o_psh = dv_alloc("pool_scale_half", 2)
    P.op("dve", lambda e: e.tensor_scalar(out=dvcol(o_psh, 2), in0=vcol("pool_scale", 0, 2), scalar1=0.5,
                                          scalar2=None, op0=ALU.mult),
         reads=[bVEC], writes=[bDV])
    sp_ = slot_of("pool_w")
    PW = RING[:, sp_, 0:2048].rearrange("p (g k n) -> p g k n", g=4, k=2)

    def pool_mm_tile(tile):
        (ti, off, n) = tile
        for g in range(4):
            for m in range(2):
                co = 2 * g + m
                bk = next_bank()
                for k in range(2):
                    P.op("pe", lambda e, bk=bk, n=n, g=g, k=k, m=m, off=off: e.matmul(
                        PS[bk][:, 0:n], lhsT=PW[:, g, k, m * 128:(m + 1) * 128], rhs=XBF[:, 2 * g + k, off:off + n],
                        start=(k == 0), stop=(k == 1)),
                        reads=[bRING[sp_], bXBF[2 * g + k][ti]], writes=[bPS[bk]], pe_acc=(k > 0))
                dst = XB[:, co, off:off + n]
                psc = dvcol(o_psh + co) if g == 0 else vcol("pool_scale", co)
                P.op("dve", lambda e, dst=dst, bk=bk, n=n, psc=psc: e.scalar_tensor_tensor(
                    out=dst, in0=PS[bk][:, 0:n], scalar=psc, in1=dst, op0=ALU.mult, op1=ALU.add),
                    reads=[bPS[bk], bXB[co][ti], bVEC, bDV], writes=[bXB[co][ti]])
                yield

    pipeline(PIPETILES, pool_mm_tile,
             lambda t: ln_s3(t, "mix_ln_g0", "mix_ln_b0", "midx"),
             lambda t: mlp_head(0, t))
    release("pool_w")
    mlp_rest(0, ALLTILES)

    def conv_in_tile(tile):
        (ti, off, n) = tile
        for c in range(NCH):
            u, cc = c // 2, c % 2
            s = slot_of(f"cwin{u}")
            W = RING[:, s, :].rearrange("p (k n) -> p k n", k=8)
            pa = next_bank()
            pg = next_bank()
            for k in range(8):
                P.op("pe", lambda e, pa=pa, n=n, k=k, cc=cc, off=off, W=W: e.matmul(
                    PS[pa][:, 0:n], lhsT=W[:, k, cc * 128:(cc + 1) * 128], rhs=XBF[:, k, off:off + n],
                    start=(k == 0), stop=(k == 7)),
                    reads=[bRING[s], bXBF[k][ti]], writes=[bPS[pa]], pe_acc=(k > 0))
            for k in range(8):
                P.op("pe", lambda e, pg=pg, n=n, k=k, cc=cc, off=off, W=W: e.matmul(
                    PS[pg][:, 0:n], lhsT=W[:, k, 256 + cc * 128:256 + (cc + 1) * 128], rhs=XBF[:, k, off:off + n],
                    start=(k == 0), stop=(k == 7)),
                    reads=[bRING[s], bXBF[k][ti]], writes=[bPS[pg]], pe_acc=(k > 0))
            si = rot("sg")
            P.op("act", lambda e, si=si, n=n, pg=pg, c=c: e.activation(
                out=SG[:, si, 0:n], in_=PS[pg][:, 0:n], func=AF.Sigmoid, bias=vcol("conv_b_in", 8 + c)),
                reads=[bPS[pg], bVEC], writes=[bSG[si]])
            P.op("dve", lambda e, si=si, n=n, pa=pa, c=c, off=off: e.scalar_tensor_tensor(
                out=HB[:, c, off:off + n], in0=PS[pa][:, 0:n], scalar=vcol("conv_b_in", c),
                in1=SG[:, si, 0:n], op0=ALU.add, op1=ALU.mult),
                reads=[bPS[pa], bSG[si], bVEC], writes=[bHB[c][ti]])
            if ti == 0:
                P.op("dve", lambda e, c=c, off=off, n=n: e.tensor_scalar(
                    out=HB[:, c, off:off + n], in0=HB[:, c, off:off + n], scalar1=PCV[:, 0:1],
                    scalar2=None, op0=ALU.mult),
                    reads=[bHB[c][ti], bPCV], writes=[bHB[c][ti]])
            yield

    pipeline(PIPETILES, lambda t: gemm2_tile(0, 3, t),
             lambda t: ln_s3(t, "mlp_ln_g0", "mlp_ln_b0", "mid", AG["mlp0"]),
             conv_in_tile)
    mlp_tail_done(0)
    for u in range(4):
        release(f"cwin{u}")

    HC = XBF.rearrange("p c t -> p (c t)").bitcast(F32)
    bHC = [[Buf(f"HC{c}_{t}") for t in range(2)] for c in range(NCH)]
    allXBF = [b for c in range(NCH) for b in bXBF[c]]
    allHC = [b for c in range(NCH) for b in bHC[c]]
    fence(allXBF + allHC)
    sSB = [slot_of("SB0"), slot_of("SB1")]
    sDG = [slot_of("DG0"), slot_of("DG1")]
    bSB = [[Buf(f"SB{c}_{t}") for t in range(2)] for c in range(NCH)]
    bDG = [Buf("DGa"), Buf("DGb")]
    allSB = [b for c in range(NCH) for b in bSB[c]]
    fence([bRING[sSB[0]], bRING[sSB[1]], bRING[sDG[0]], bRING[sDG[1]]] + allSB + bDG)
    SBv = [RING[:, sSB[i], :].rearrange("p (c t n) -> p c t n", c=4, t=2) for i in range(2)]
    DGv = [RING[:, sDG[i], 0:KW * 128].rearrange("p (k n) -> p k n", k=KW) for i in range(2)]
    so0 = slot_of("cwout0")
    so1 = slot_of("cwout1")
    Wo = [RING[:, so0, :].rearrange("p (k n) -> p k n", k=8), RING[:, so1, :].rearrange("p (k n) -> p k n", k=8)]
    bWo = [bRING[so0], bRING[so1]]
    dwo, _ = VEC_LAYOUT["conv_dw"]
    ido, _ = VEC_LAYOUT["ident"]
    dgi = [0]

    def hc_ap(c, tl):
        o = (c * 2 + tl) * 512
        return HC[:, o:o + 512]

    def w_out_tile(tile, tl):
        (ti, off, n) = tile
        for m in range(NCH):
            bk = next_bank()
            for k in range(8):
                P.op("pe", lambda e, bk=bk, k=k, m=m, tl=tl: e.matmul(
                    PS[bk][:, 0:512], lhsT=Wo[m // 4][:, k, (m % 4) * 128:(m % 4 + 1) * 128],
                    rhs=SBv[k // 4][:, k % 4, tl, :], start=(k == 0), stop=(k == 7)),
                    reads=[bWo[m // 4], bSB[k][tl]], writes=[bPS[bk]], pe_acc=(k > 0))
            dst = XB[:, m, off:off + 512]
            P.op("dve", lambda e, dst=dst, bk=bk: e.tensor_tensor(out=dst, in0=PS[bk][:, 0:512], in1=dst, op=ALU.add),
                 reads=[bPS[bk], bXB[m][ti]], writes=[bXB[m][ti]])
            yield

    dg_seq = [(t, c) for t in range(4) for c in range(NCH)]

    def build_dg(i):
        c = dg_seq[i][1]
        di = i % 2
        P.op("dve", lambda e, di=di, c=c: e.tensor_tensor(
            out=DGv[di], in0=VEC[:, ido:ido + 128].unsqueeze(1).to_broadcast([128, KW, 128]),
            in1=VEC[:, dwo + c * KW:dwo + (c + 1) * KW].unsqueeze(2).to_broadcast([128, KW, 128]),
            op=ALU.mult),
            reads=[bVEC], writes=[bDG[di]])

    def conv_ln_chain(par):
        ln_apply_stats(par, 4 + 2 * par, 5 + 2 * par, 512)

    def conv_ln_apply(par):
        for c in range(NCH):
            hca = hc_ap(c, par)
            P.op("dve", lambda e, hca=hca, par=par: e.tensor_tensor(out=hca, in0=hca, in1=LNB[:, par, :], op=ALU.mult),
                 reads=[bHC[c][par], bLNB[par]], writes=[bHC[c][par]])
        for c in range(NCH):
            hca = hc_ap(c, par)
            P.op("dve", lambda e, hca=hca, par=par: e.tensor_tensor(out=hca, in0=hca, in1=LNA[:, par, :], op=ALU.add),
                 reads=[bHC[c][par], bLNA[par]], writes=[bHC[c][par]])

    def conv_silu(par):
        for c in range(NCH):
            hca = hc_ap(c, par)
            P.op("act", lambda e, hca=hca, c=c, par=par: e.activation(
                out=SBv[c // 4][:, c % 4, par, :], in_=hca, func=AF.Silu,
                bias=vcol("conv_ln_b", c), scale=vcol("conv_ln_g", c)),
                reads=[bHC[c][par], bVEC], writes=[bSB[c][par]])

    def conv_side(tprev):
        for _ in conv_side_ln(tprev):
            yield
        for _ in w_out_tile(MTILES[tprev], tprev % 2):
            yield

    def conv_side_ln(tprev):
        pp = tprev % 2
        bmp, bep = 4 + 2 * pp, 5 + 2 * pp
        bb = LNB[:, pp, :]
        P.op("act", lambda e: e.activation(out=bb, in_=PS[bmp][:, 0:512], func=AF.Square),
             reads=[bPS[bmp]], writes=[bLNB[pp]])
        P.op("dve", lambda e: e.scalar_tensor_tensor(out=bb, in0=PS[bep][:, 0:512], scalar=EPS, in1=bb,
                                                     op0=ALU.add, op1=ALU.subtract),
             reads=[bPS[bep], bLNB[pp]], writes=[bLNB[pp]])
        P.op("act", lambda e: e.activation(out=bb, in_=bb, func=AF.Ln),
             reads=[bLNB[pp]], writes=[bLNB[pp]])
        P.op("act", lambda e: e.activation(out=bb, in_=bb, func=AF.Exp, scale=-0.5),
             reads=[bLNB[pp]], writes=[bLNB[pp]])
        yield
        for c in range(NCH):
            hca = hc_ap(c, pp)
            P.op("dve", lambda e, hca=hca, bmp=bmp: e.tensor_tensor(out=hca, in0=hca, in1=PS[bmp][:, 0:512], op=ALU.subtract),
                 reads=[bHC[c][pp], bPS[bmp]], writes=[bHC[c][pp]])
            if c % 2 == 1:
                yield
        for c in range(NCH):
            hca = hc_ap(c, pp)
            P.op("dve", lambda e, hca=hca, pp=pp: e.tensor_tensor(out=hca, in0=hca, in1=LNB[:, pp, :], op=ALU.mult),
                 reads=[bHC[c][pp], bLNB[pp]], writes=[bHC[c][pp]])
            P.op("act", lambda e, hca=hca, c=c, pp=pp: e.activation(
                out=SBv[c // 4][:, c % 4, pp, :], in_=hca, func=AF.Silu,
                bias=vcol("conv_ln_b", c), scale=vcol("conv_ln_g", c)),
                reads=[bHC[c][pp], bVEC], writes=[bSB[c][pp]])
            if c % 2 == 1:
                yield

    def advance(gen, k):
        if gen is None:
            return
        for _ in range(k):
            try:
                next(gen)
            except StopIteration:
                return

    gemm_banks[0] = [0, 1, 2, 3]
    gemm_bank[0] = 0
    st_ctr = [0]
    build_dg(0)
    for t in range(4):
        (ti, off, n) = MTILES[t]
        par = t % 2
        side = conv_side(t - 1) if t > 0 else None
        deferred = []
        for c in range(NCH):
            i = t * NCH + c
            di = i % 2
            if i + 1 < len(dg_seq):
                build_dg(i + 1)
            bk = next_bank()
            for k in range(KW):
                P.op("pe", lambda e, bk=bk, di=di, k=k, c=c, off=off: e.matmul(
                    PS[bk][:, 0:512], lhsT=DGv[di][:, k, :], rhs=HB[:, c, off - 30 + k:off - 30 + k + 512],
                    start=(k == 0), stop=(k == KW - 1)),
                    reads=[bDG[di], bHB[c][ti], bHB[c][ti - 1]], writes=[bPS[bk]], pe_acc=(k > 0))
            while len(deferred) >= 2:
                deferred.pop(0)()
            hca = hc_ap(c, par)
            P.op("act", lambda e, hca=hca, bk=bk, c=c: e.activation(
                out=hca, in_=PS[bk][:, 0:512], func=AF.Identity, bias=vcol("conv_dw_b", c)),
                reads=[bPS[bk], bVEC], writes=[bHC[c][par]])
            st_ctr[0] = (st_ctr[0] + 1) % 3
            si = st_ctr[0]
            P.op("act", lambda e, si=si, bk=bk, c=c: e.activation(
                out=VSQ[:, si, :], in_=PS[bk][:, 0:512], func=AF.Square, bias=vcol("conv_dw_b", c)),
                reads=[bPS[bk], bVEC], writes=[bVSQ[si]])
            vi = si
            P.op("act", lambda e, vi=vi, bk=bk, c=c: e.activation(
                out=VBT[:, vi, :], in_=PS[bk][:, 0:512], func=AF.Identity, bias=vcol("conv_dw_b", c)),
                reads=[bPS[bk], bVEC], writes=[bVBT[vi]])
            bm, be = 4 + 2 * par, 5 + 2 * par

            def stats(vi=vi, si=si, bm=bm, be=be, c=c):
                P.op("pe", lambda e: e.matmul(PS[bm][:, 0:512], lhsT=ONES, rhs=VBT[:, vi, :],
                                              start=(c == 0), stop=(c == NCH - 1)),
                     reads=[bONES, bVBT[vi]], writes=[bPS[bm]], pe_acc=(c > 0))
                P.op("pe", lambda e: e.matmul(PS[be][:, 0:512], lhsT=ONES, rhs=VSQ[:, si, :],
                                              start=(c == 0), stop=(c == NCH - 1)),
                     reads=[bONES, bVSQ[si]], writes=[bPS[be]], pe_acc=(c > 0))
            deferred.append(stats)
            advance(side, 2 if c < 4 else (3 if c < 7 else 99))
        for f in deferred:
            f()
    fence([bRING[sDG[0]], bRING[sDG[1]]] + bDG)
    release("DG0")
    release("DG1")
    run(conv_side_ln(3))
    gemm_banks[0] = [0, 1, 2, 3, 6, 7]
    fence(allXBF + allHC)

    def x3(tile):
        if tile[0] == 4:
            return w_out_tile(tile, 1)
        return None

    pipeline(MTILES, x3,
             lambda t: ln_s3(t, "mix_ln_g1", "mix_ln_b1", "midx"),
             lambda t: mlp_head(1, t))
    fence([bRING[sSB[0]], bRING[sSB[1]]] + allSB)
    release("cwout0")
    release("cwout1")
    release("SB0")
    release("SB1")
    mlp_rest(1, MTILES)
    pipeline(MTILES, lambda t: gemm2_tile(1, 3, t),
             lambda t: ln_s3(t, "mlp_ln_g1", "mlp_ln_b1", "final"), None)
    mlp_tail_done(1)
    return finish()


def _emit(nc, P):
    P.resolve()
    from contextlib import ExitStack
    with ExitStack() as es:
        sems = {}
        for name in P.sem_names:
            sems[name] = es.enter_context(nc.semaphore(name))
        block = es.enter_context(nc.Block())
        outsem_total = P.final.get("out", 0)

        @block.tensor
        def _(eng):
            P.emit_engine("pe", eng, sems)

        @block.scalar
        def _(eng):
            P.emit_engine("act", eng, sems)

        @block.vector
        def _(eng):
            P.emit_engine("dve", eng, sems)

        @block.gpsimd
        def _(eng):
            P.emit_engine("pool", eng, sems)
            for name in P.sem_names:
                if name.startswith("ring"):
                    eng.wait_ge(sems[name], P.final[name])

        @block.sync
        def _(eng):
            P.emit_engine("sp", eng, sems)
            for name in P.sem_names:
                if name in ("vec", "pcv") or name.startswith("xin"):
                    eng.wait_ge(sems[name], P.final[name])
            eng.wait_ge(sems["out"], outsem_total)
    return nc


def _fm(v, n):
    return np.ascontiguousarray(np.asarray(v, np.float32).reshape(n, 128).T)


def _pack_vecs(inp):
    V = np.zeros((128, NV), np.float32)

    def put(name, arr):
        o, n = VEC_LAYOUT[name]
        assert arr.shape == (128, n), (name, arr.shape, n)
        V[:, o:o + n] = arr

    put("pool_scale", _fm(inp["pool_scale"][0], 8))
    for l in range(2):
        put(f"mix_ln_g{l}", _fm(inp["mix_ln_g"][l], 8))
        put(f"mix_ln_b{l}", _fm(inp["mix_ln_b"][l], 8))
        put(f"mlp_b1_{l}", _fm(inp["mlp_b1"][l], 32))
        put(f"mlp_b2_{l}", _fm(inp["mlp_b2"][l], 8))
        put(f"mlp_ln_g{l}", _fm(inp["mlp_ln_g"][l], 8))
        put(f"mlp_ln_b{l}", _fm(inp["mlp_ln_b"][l], 8))
    put("conv_b_in", _fm(inp["conv_b_in"][0], 16))
    dw = np.asarray(inp["conv_dw"][0], np.float32)
    dwl = dw.reshape(KW, 8, 128).transpose(2, 1, 0).reshape(128, 8 * KW)
    put("conv_dw", np.ascontiguousarray(dwl))
    put("conv_dw_b", _fm(inp["conv_dw_b"][0], 8))
    put("conv_ln_g", _fm(inp["conv_ln_g"][0], 8))
    put("conv_ln_b", _fm(inp["conv_ln_b"][0], 8))
    put("conv_b_out", _fm(inp["conv_b_out"][0], 8))
    put("ident", np.eye(128, dtype=np.float32))
    return V


def _percore_tables():
    tabs = []
    for core in range(NCORES):
        first = (core % 4 == 0)
        t = np.zeros((128, NPC), np.float32)
        t[:, 0] = 0.0 if first else 1.0
        for g, w in enumerate(POOLW):
            for i in range(16):
                cntv = min(i + 1, w) if first else w
                t[:, 1 + 16 * g + i] = np.float32(1.0) / np.float32(cntv)
        tabs.append(t)
    return tabs


_CACHE = {}


def _get_nc(dbg_stage=None):
    if dbg_stage not in _CACHE:
        nc, P = build_program(dbg_stage)
        _CACHE[dbg_stage] = _emit(nc, P)
    return _CACHE[dbg_stage]


def _in_maps(inp):
    x = np.asarray(inp["x"], np.float32)
    V = _pack_vecs(inp)
    tabs = _percore_tables()
    shared = {
        "vecs": V,
        "pool_w": np.ascontiguousarray(np.asarray(inp["pool_w"], np.float32).reshape(1024, 256)),
        "conv_w_in": np.ascontiguousarray(np.asarray(inp["conv_w_in"], np.float32).reshape(1024, 2048)),
        "conv_w_out": np.ascontiguousarray(np.asarray(inp["conv_w_out"], np.float32).reshape(1024, 1024)),
        "mlp_w1": np.ascontiguousarray(np.asarray(inp["mlp_w1"], np.float32).reshape(2048, DFF)),
        "mlp_w2": np.ascontiguousarray(np.asarray(inp["mlp_w2"], np.float32).reshape(2 * DFF, 1024)),
    }
    maps = []
    for core in range(NCORES):
        b, ch = core // 4, core % 4
        t0 = ch * TPC
        xs = np.zeros((TT, D), np.float32)
        lo = max(0, t0 - HALO)
        xs[HALO - (t0 - lo):, :] = x[b, lo:t0 + TPC, :]
        m = dict(shared)
        m["xT"] = np.ascontiguousarray(xs.T)
        m["pcore"] = tabs[core]
        maps.append(m)
    return maps


def kernel(**inputs):
    nc = _get_nc(None)
    maps = _in_maps(inputs)
    res = run_bass_kernel_spmd(nc, maps, core_ids=list(range(NCORES)))
    out = np.empty((BATCH, SEQ, D), np.float32)
    for core in range(NCORES):
        b, ch = core // 4, core % 4
        y = np.asarray(res.results[core]["yT"]).reshape(4, NCH, 128, 512)
        out[b, ch * TPC:(ch + 1) * TPC, :] = y.transpose(0, 3, 1, 2).reshape(TPC, D)
    return out
```

```python
import numpy as np
import concourse.bass as bass
import concourse.mybir as mybir
from concourse.bass_utils import run_bass_kernel_spmd

F32 = mybir.dt.float32
BF16 = mybir.dt.bfloat16
AF = mybir.ActivationFunctionType
ALU = mybir.AluOpType

NCORES = 8
D = 1024
NCH = 8
DFF = 4096
SEQ = 8192
BATCH = 2
TPC = 2048
HALO = 64
TT = HALO + TPC
KW = 31
ALPHA = 4.0 ** 0.25
EPS = 1e-5
POOLW = (2, 4, 8, 16)
RSLOTS = 6
XRES_ENG = "dve"
SLOT_ELEMS = 4096

HTILE = (0, 32, 32)
MTILES = [(1 + i, HALO + 512 * i, 512) for i in range(4)]
ALLTILES = [HTILE] + MTILES
PIPETILES = MTILES + [HTILE]

VEC_LAYOUT = {}
_off = 0


def _reg(name, n):
    global _off
    VEC_LAYOUT[name] = (_off, n)
    _off += n


_reg("pool_scale", 8)
for _l in range(2):
    _reg(f"mix_ln_g{_l}", 8)
    _reg(f"mix_ln_b{_l}", 8)
    _reg(f"mlp_b1_{_l}", 32)
    _reg(f"mlp_b2_{_l}", 8)
    _reg(f"mlp_ln_g{_l}", 8)
    _reg(f"mlp_ln_b{_l}", 8)
_reg("conv_b_in", 16)
_reg("conv_dw", 8 * KW)
_reg("conv_dw_b", 8)
_reg("conv_ln_g", 8)
_reg("conv_ln_b", 8)
_reg("conv_b_out", 8)
_reg("ident", 128)
NV = _off
NPC = 1 + 64


class Buf:
    __slots__ = ("name", "lw", "rd")

    def __init__(self, name):
        self.name = name
        self.lw = None
        self.rd = {}


class Op:
    __slots__ = ("eng", "fn", "reads", "writes", "sem", "ndma", "pe_acc", "idx", "waits", "sig")

    def __init__(self, eng, fn, reads, writes, sem=None, ndma=0, pe_acc=False):
        self.eng = eng
        self.fn = fn
        self.reads = reads
        self.writes = writes
        self.sem = sem
        self.ndma = ndma
        self.pe_acc = pe_acc
        self.idx = None
        self.waits = None
        self.sig = None


COMPUTE = ("pe", "act", "dve", "pool")
QUEUES = ("sp",)


class Prog:
    def __init__(self):
        self.ops = []

    def op(self, eng, fn, reads=(), writes=(), pe_acc=False):
        self.ops.append(Op(eng, fn, list(reads), list(writes), pe_acc=pe_acc))

    def dma(self, queue, sem, fn, reads=(), writes=(), ndma=1):
        self.ops.append(Op(queue, fn, list(reads), list(writes), sem=sem, ndma=ndma))

    def resolve(self):
        known = {}
        cnt = {}
        snaps = {}
        needed = set()
        for op in self.ops:
            E = op.eng
            S = op.sem if op.sem is not None else E
            kn = known.setdefault(E, {})
            deps = set()
            for b in op.reads:
                if b.lw is not None:
                    deps.add(b.lw)
            for b in op.writes:
                if b.lw is not None:
                    if not (op.pe_acc and b.lw[0] == "pe"):
                        deps.add(b.lw)
                for it in b.rd.items():
                    deps.add(it)
            waits = []
            for (De, Di) in sorted(deps, key=lambda t: -t[1]):
                if kn.get(De, -1) >= Di:
                    continue
                waits.append((De, Di))
                needed.add((De, Di))
                kn[De] = Di
                for k2, v2 in snaps[(De, Di)].items():
                    if kn.get(k2, -1) < v2:
                        kn[k2] = v2
            idx = cnt.get(S, 0)
            cnt[S] = idx + 1
            op.idx = idx
            op.waits = waits
            op.sig = S
            snaps[(S, idx)] = dict(kn)
            for b in op.reads:
                b.rd[S] = idx
            for b in op.writes:
                b.lw = (S, idx)
                b.rd = {}
        self.needed = needed
        self.val = {}
        run = {}
        for op in self.ops:
            S = op.sig
            if op.sem is not None:
                run[S] = run.get(S, 0) + 16 * op.ndma
                self.val[(S, op.idx)] = run[S]
            else:
                if (S, op.idx) in needed:
                    run[S] = run.get(S, 0) + 1
                    self.val[(S, op.idx)] = run[S]
        self.sem_names = sorted(cnt.keys())
        self.final = dict(run)

    def emit_engine(self, E, handle, sems):
        for op in self.ops:
            if op.eng != E:
                continue
            for (De, Di) in op.waits:
                handle.wait_ge(sems[De], self.val[(De, Di)])
            r = op.fn(handle)
            if op.sem is not None:
                insts = r if isinstance(r, (list, tuple)) else [r]
                assert len(insts) == op.ndma
                for i in insts:
                    i.then_inc(sems[op.sem], 16)
            elif (op.sig, op.idx) in self.needed:
                r.then_inc(sems[op.sig], 1)


def build_program(dbg_stage=None):
    nc = bass.Bass("TRN2", target_bir_lowering=False)
    P = Prog()

    xT = nc.dram_tensor("xT", [D, TT], F32, kind="ExternalInput").ap()
    vecs_d = nc.dram_tensor("vecs", [128, NV], F32, kind="ExternalInput").ap()
    pc_d = nc.dram_tensor("pcore", [128, NPC], F32, kind="ExternalInput").ap()
    pool_w_d = nc.dram_tensor("pool_w", [1024, 256], F32, kind="ExternalInput").ap()
    cwin_d = nc.dram_tensor("conv_w_in", [1024, 2048], F32, kind="ExternalInput").ap()
    cwout_d = nc.dram_tensor("conv_w_out", [1024, 1024], F32, kind="ExternalInput").ap()
    w1_d = nc.dram_tensor("mlp_w1", [2 * 1024, DFF], F32, kind="ExternalInput").ap()
    w2_d = nc.dram_tensor("mlp_w2", [2 * DFF, 1024], F32, kind="ExternalInput").ap()
    yT = nc.dram_tensor("yT", [4 * NCH * 128, 512], F32, kind="ExternalOutput").ap()
    dbg_d = None
    if dbg_stage is not None:
        dbg_d = nc.dram_tensor("dbg", [D, TT], F32, kind="ExternalOutput").ap()

    A = nc.alloc_sbuf_tensor
    XB = A("XB", [128, NCH, TT], F32).ap()
    XBF = A("XBF", [128, NCH, TT], BF16).ap()
    HB = A("HB", [128, NCH, TT], BF16).ap()
    RING = A("RING", [128, RSLOTS, SLOT_ELEMS], BF16).ap()
    VEC = A("VEC", [128, NV], F32).ap()
    PCV = A("PCV", [128, NPC], F32).ap()
    DV = A("DV", [128, 64], F32).ap()
    ONES = A("ONES", [128, 128], BF16).ap()
    ONES32 = A("ONES32", [128, 128], F32).ap()
    SG = A("SG", [128, 2, 512], F32).ap()
    VBT = A("VBT", [128, 3, 512], BF16).ap()
    VSQ = A("VSQ", [128, 3, 512], BF16).ap()
    RT = A("RT", [128, 4, 512], BF16).ap()
    LNA = A("LNA", [128, 2, 512], F32).ap()
    LNB = A("LNB", [128, 2, 512], F32).ap()
    PS = [nc.alloc_psum_tensor(f"PS{i}", [128, 512], F32).ap() for i in range(8)] \
        if hasattr(nc, "alloc_psum_tensor") else None
    assert PS is not None

    bXB = [[Buf(f"XB{c}_{t}") for t in range(5)] for c in range(NCH)]
    bXBF = [[Buf(f"XBF{c}_{t}") for t in range(5)] for c in range(NCH)]
    bHB = [[Buf(f"HB{c}_{t}") for t in range(5)] for c in range(NCH)]
    bRING = [Buf(f"RING{s}") for s in range(RSLOTS)]
    bVEC, bPCV, bDV, bONES = Buf("VEC"), Buf("PCV"), Buf("DV"), Buf("ONES")
    bSG = [Buf("SG0"), Buf("SG1")]
    bVBT = [Buf(f"VBT{i}") for i in range(3)]
    bVSQ = [Buf(f"VSQ{i}") for i in range(3)]
    bRT = [Buf(f"RT{i}") for i in range(4)]
    bLNA = [Buf("LNA0"), Buf("LNA1")]
    bLNB = [Buf("LNB0"), Buf("LNB1")]
    bPS = [Buf(f"PS{i}") for i in range(8)]
    bOUT = Buf("OUT")

    def vcol(name, j=0, n=1):
        o, _ = VEC_LAYOUT[name]
        return VEC[:, o + j:o + j + n]

    def dvcol(j, n=1):
        return DV[:, j:j + n]

    DV_AG = {}
    _dvo = [0]

    def dv_alloc(name, n=8):
        DV_AG[name] = _dvo[0]
        _dvo[0] += n
        return DV_AG[name]

    free_slots = list(range(RSLOTS))
    pending = []
    granted = {}

    def request(name, loader):
        pending.append((name, loader))
        pump()

    def pump():
        while pending and free_slots:
            name, loader = pending.pop(0)
            s = free_slots.pop(0)
            granted[name] = s
            if loader is not None:
                loader(s)

    def release(name):
        s = granted.pop(name)
        free_slots.append(s)
        pump()

    def slot_of(name):
        assert name in granted, f"unit {name} not granted (ring too small?)"
        return granted[name]

    ring_dep = []

    def load_cols(src2d, row0, col0, ncols):
        def loader(s):
            dst = RING[:, s, 0:8 * ncols].rearrange("p (k n) -> p k n", k=8)
            src = src2d[row0:row0 + 1024, col0:col0 + ncols].rearrange("(k p) n -> p k n", p=128)
            P.dma("pool", f"ring{s}", lambda g, dst=dst, src=src: g.dma_start(out=dst, in_=src),
                  reads=list(ring_dep), writes=[bRING[s]])
        return loader

    def load_rows(src2d, row0, nk, ncols):
        def loader(s):
            dst = RING[:, s, 0:nk * ncols].rearrange("p (k n) -> p k n", k=nk)
            src = src2d[row0:row0 + nk * 128, 0:ncols].rearrange("(k p) n -> p k n", p=128)
            P.dma("pool", f"ring{s}", lambda g, dst=dst, src=src: g.dma_start(out=dst, in_=src),
                  reads=list(ring_dep), writes=[bRING[s]])
        return loader

    def load_glu(u):
        def loader(s):
            dst = RING[:, s, :].rearrange("p (k n) -> p k n", k=8)
            sa = cwin_d[:, 256 * u:256 * u + 256].rearrange("(k p) n -> p k n", p=128)
            sg = cwin_d[:, 1024 + 256 * u:1024 + 256 * u + 256].rearrange("(k p) n -> p k n", p=128)
            P.dma("pool", f"ring{s}",
                  lambda g: [g.dma_start(out=dst[:, :, 0:256], in_=sa),
                             g.dma_start(out=dst[:, :, 256:512], in_=sg)],
                  reads=list(ring_dep), writes=[bRING[s]], ndma=2)
        return loader

    P.dma("sp", "vec", lambda e: e.dma_start(out=VEC, in_=vecs_d), writes=[bVEC])
    P.dma("sp", "pcv", lambda e: e.dma_start(out=PCV, in_=pc_d), writes=[bPCV])
    for c in range(NCH):
        P.dma("sp", f"xin{c}",
              lambda e, c=c: e.dma_start(out=XB[:, c, :], in_=xT[c * 128:(c + 1) * 128, :]),
              writes=bXB[c])
    P.op("dve", lambda e: e.memset(ONES, 1.0 / 1024.0), writes=[bONES])
    bONES32 = Buf("ONES32")
    P.op("dve", lambda e: e.memset(ONES32, 1.0 / 1024.0), writes=[bONES32])

    def pool_loader(s):
        dst = RING[:, s, 0:2048].rearrange("p (g k n) -> p g k n", g=4, k=2)
        src = pool_w_d.rearrange("(g k p) n -> p g k n", g=4, k=2)
        P.dma("pool", f"ring{s}", lambda g: g.dma_start(out=dst, in_=src), writes=[bRING[s]])

    def queue_mlp_units(l):
        for q in range(4):
            for h in range(2):
                request(f"w1_{l}_{2 * q + h}", load_cols(w1_d, l * 1024, (2 * q + h) * 512, 512))
            for h in range(2):
                request(f"w2_{l}_{2 * q + h}", load_rows(w2_d, l * DFF + (2 * q + h) * 512, 4, 1024))

    def derive(gname, bname, nextbias):
        og = dv_alloc(gname + "_ag")
        ob = dv_alloc(bname + "_ab")
        P.op("dve", lambda e: e.tensor_scalar(out=dvcol(og, 8), in0=vcol(gname, 0, 8), scalar1=ALPHA,
                                              scalar2=None, op0=ALU.mult),
             reads=[bVEC], writes=[bDV])
        P.op("dve", lambda e: e.scalar_tensor_tensor(out=dvcol(ob, 8), in0=vcol(bname, 0, 8), scalar=ALPHA,
                                                     in1=vcol(nextbias, 0, 8), op0=ALU.mult, op1=ALU.add),
             reads=[bVEC], writes=[bDV])
        return og, ob

    AG = {}
    AG["mix0"] = derive("mix_ln_g0", "mix_ln_b0", "mlp_b2_0")
    AG["mlp0"] = derive("mlp_ln_g0", "mlp_ln_b0", "conv_b_out")
    AG["mix1"] = derive("mix_ln_g1", "mix_ln_b1", "mlp_b2_1")

    gemm_bank = [0]
    gemm_banks = [[0, 1, 2, 3, 6, 7]]

    def next_bank():
        lst = gemm_banks[0]
        gemm_bank[0] = (gemm_bank[0] + 1) % len(lst)
        return lst[gemm_bank[0]]

    tmp_rot = {"sg": 0, "vbt": 0, "vsq": 0, "rt": 0, "ln": 0}
    rt_ctr = [0]

    def rot(name):
        v = tmp_rot[name]
        tmp_rot[name] = 1 - v
        return v

    ln_set = {}

    def ln_s1(tile):
        (ti, off, n) = tile
        for c in range(NCH):
            src = XB[:, c, off:off + n]
            P.op("act", lambda e, c=c, src=src, off=off, n=n: e.activation(out=HB[:, c, off:off + n], in_=src, func=AF.Square),
                 reads=[bXB[c][ti]], writes=[bHB[c][ti]])
            P.op("dve", lambda e, c=c, src=src, off=off, n=n: e.tensor_copy(out=XBF[:, c, off:off + n], in_=src),
                 reads=[bXB[c][ti]], writes=[bXBF[c][ti]])
            yield

    def ln_s2(tile):
        (ti, off, n) = tile
        st = rot("ln")
        ln_set[ti] = st
        bm, be = 4, 5
        for c in range(NCH):
            P.op("pe", lambda e, n=n, bm=bm, c=c, off=off: e.matmul(PS[bm][:, 0:n], lhsT=ONES, rhs=XBF[:, c, off:off + n],
                                                                   start=(c == 0), stop=(c == NCH - 1)),
                 reads=[bONES, bXBF[c][ti]], writes=[bPS[bm]], pe_acc=(c > 0))
            P.op("pe", lambda e, n=n, be=be, c=c, off=off: e.matmul(PS[be][:, 0:n], lhsT=ONES, rhs=HB[:, c, off:off + n],
                                                                   start=(c == 0), stop=(c == NCH - 1)),
                 reads=[bONES, bHB[c][ti]], writes=[bPS[be]], pe_acc=(c > 0))
        b = LNB[:, st, 0:n]
        P.op("act", lambda e: e.activation(out=b, in_=PS[bm][:, 0:n], func=AF.Square),
             reads=[bPS[bm]], writes=[bLNB[st]])
        P.op("dve", lambda e: e.scalar_tensor_tensor(out=b, in0=PS[be][:, 0:n], scalar=EPS, in1=b,
                                                     op0=ALU.add, op1=ALU.subtract),
             reads=[bPS[be], bLNB[st]], writes=[bLNB[st]])
        P.op("act", lambda e: e.activation(out=b, in_=b, func=AF.Ln),
             reads=[bLNB[st]], writes=[bLNB[st]])
        P.op("act", lambda e: e.activation(out=b, in_=b, func=AF.Exp, scale=-0.5),
             reads=[bLNB[st]], writes=[bLNB[st]])

    def ln_s3(tile, gname, bname, mode, ag=None, xres_eng="dve"):
        (ti, off, n) = tile
        st = ln_set[ti]
        bm = 4
        b = LNB[:, st, 0:n]
        for c in range(NCH):
            dst = XB[:, c, off:off + n]
            P.op("dve", lambda e, dst=dst, bm=bm, n=n: e.tensor_tensor(out=dst, in0=dst, in1=PS[bm][:, 0:n], op=ALU.subtract),
                 reads=[bXB[c][ti], bPS[bm]], writes=[bXB[c][ti]])
            yield
        for c in range(NCH):
            dst = XB[:, c, off:off + n]
            P.op("dve", lambda e, dst=dst, st=st, n=n: e.tensor_tensor(out=dst, in0=dst, in1=LNB[:, st, 0:n], op=ALU.mult),
                 reads=[bXB[c][ti], bLNB[st]], writes=[bXB[c][ti]])
            if mode in ("mid", "midx"):
                P.op("act", lambda e, dst=dst, c=c, off=off, n=n: e.activation(
                    out=XBF[:, c, off:off + n], in_=dst, func=AF.Identity,
                    bias=vcol(bname, c), scale=vcol(gname, c)),
                    reads=[bXB[c][ti], bVEC], writes=[bXBF[c][ti]])
            yield
        if mode == "midx":
            return
        for c in range(NCH):
            dst = XB[:, c, off:off + n]
            if mode == "mid":
                og, ob = ag
                if xres_eng == "act":
                    P.op("act", lambda e, dst=dst, c=c, og=og, ob=ob: e.activation(
                        out=dst, in_=dst, func=AF.Identity, bias=dvcol(ob + c), scale=dvcol(og + c)),
                        reads=[bXB[c][ti], bDV], writes=[bXB[c][ti]])
                else:
                    P.op("dve", lambda e, dst=dst, c=c, og=og, ob=ob: e.tensor_scalar(
                        out=dst, in0=dst, scalar1=dvcol(og + c), scalar2=dvcol(ob + c), op0=ALU.mult, op1=ALU.add),
                        reads=[bXB[c][ti], bDV], writes=[bXB[c][ti]])
            else:
                P.op("act", lambda e, dst=dst, c=c: e.activation(
                    out=dst, in_=dst, func=AF.Identity, bias=vcol(bname, c), scale=vcol(gname, c)),
                    reads=[bXB[c][ti], bVEC], writes=[bXB[c][ti]])
                t0 = off - HALO
                if c % 2 == 1:
                    r0 = ((t0 // 512) * NCH + c - 1) * 128
                    P.dma("sp", "out", lambda e, c=c, r0=r0, off=off, n=n: e.dma_start(
                        out=yT[r0:r0 + 256, 0:n].rearrange("(c p) n -> p c n", p=128),
                        in_=XB[:, c - 1:c + 1, off:off + n]),
                        reads=[bXB[c - 1][ti], bXB[c][ti]], writes=[bOUT])

    def run(gen):
        if gen is not None:
            for _ in gen:
                pass

    def merge(a, b, ra=1, rb=1):
        a = iter(a) if a is not None else iter(())
        b = iter(b) if b is not None else iter(())
        da = db = False
        while not (da and db):
            for _ in range(ra):
                if not da:
                    try:
                        next(a)
                    except StopIteration:
                        da = True
            for _ in range(rb):
                if not db:
                    try:
                        next(b)
                    except StopIteration:
                        db = True

    def pipeline(tiles, X, S3, Y):
        T = len(tiles)
        for s_ in range(T + 2):
            gx = X(tiles[s_]) if (s_ < T and X is not None) else None
            g3 = S3(tiles[s_ - 1]) if 0 <= s_ - 1 < T else None
            merge(gx, g3, 1, 2)
            g1 = ln_s1(tiles[s_]) if s_ < T else None
            gy = Y(tiles[s_ - 2]) if (0 <= s_ - 2 < T and Y is not None) else None
            merge(g1, gy, 3, 2)
            if s_ < T:
                ln_s2(tiles[s_])

    def ln_apply_stats(st, bm, be, n):
        a = LNA[:, st, 0:n]
        b = LNB[:, st, 0:n]
        P.op("act", lambda e: e.activation(out=a, in_=PS[bm][:, 0:n], func=AF.Identity),
             reads=[bPS[bm]], writes=[bLNA[st]])
        P.op("act", lambda e: e.activation(out=b, in_=PS[bm][:, 0:n], func=AF.Square),
             reads=[bPS[bm]], writes=[bLNB[st]])
        P.op("dve", lambda e: e.scalar_tensor_tensor(out=b, in0=PS[be][:, 0:n], scalar=EPS, in1=b,
                                                     op0=ALU.add, op1=ALU.subtract),
             reads=[bPS[be], bLNB[st]], writes=[bLNB[st]])
        P.op("act", lambda e: e.activation(out=b, in_=b, func=AF.Ln),
             reads=[bLNB[st]], writes=[bLNB[st]])
        P.op("act", lambda e: e.activation(out=b, in_=b, func=AF.Exp, scale=-0.5),
             reads=[bLNB[st]], writes=[bLNB[st]])
        P.op("dve", lambda e: e.scalar_tensor_tensor(out=a, in0=a, scalar=-1.0, in1=b, op0=ALU.mult, op1=ALU.mult),
             reads=[bLNA[st], bLNB[st]], writes=[bLNA[st]])

    def gemm1_group(l, q, j, tile):
        (ti, off, n) = tile
        b1 = f"mlp_b1_{l}"
        uname = f"w1_{l}_{2 * q + j // 4}"
        s = slot_of(uname)
        W = RING[:, s, :].rearrange("p (k n) -> p k n", k=8)
        co = (j % 4) * 128
        J = 8 * q + j
        bk = next_bank()
        for k in range(8):
            P.op("pe", lambda e, bk=bk, n=n, W=W, k=k, co=co, off=off: e.matmul(
                PS[bk][:, 0:n], lhsT=W[:, k, co:co + 128], rhs=XBF[:, k, off:off + n],
                start=(k == 0), stop=(k == 7)),
                reads=[bRING[s], bXBF[k][ti]], writes=[bPS[bk]], pe_acc=(k > 0))
        rt_ctr[0] = (rt_ctr[0] + 1) % 4
        ri = rt_ctr[0]
        P.op("act", lambda e, ri=ri, n=n, bk=bk, J=J, b1=b1: e.activation(
            out=RT[:, ri, 0:n], in_=PS[bk][:, 0:n], func=AF.Relu, bias=vcol(b1, J)),
            reads=[bPS[bk], bVEC], writes=[bRT[ri]])
        P.op("dve", lambda e, ri=ri, n=n, j=j, off=off: e.tensor_tensor(
            out=HB[:, j, off:off + n], in0=RT[:, ri, 0:n], in1=RT[:, ri, 0:n], op=ALU.mult),
            reads=[bRT[ri]], writes=[bHB[j][ti]])

    def gemm2_tile(l, q, tile):
        (ti, off, n) = tile
        s0 = slot_of(f"w2_{l}_{2 * q}")
        s1 = slot_of(f"w2_{l}_{2 * q + 1}")
        Ws = [RING[:, s0, :].rearrange("p (k n) -> p k n", k=4),
              RING[:, s1, :].rearrange("p (k n) -> p k n", k=4)]
        bs = [bRING[s0], bRING[s1]]
        for m in range(8):
            bk = next_bank()
            for j in range(8):
                P.op("pe", lambda e, bk=bk, n=n, j=j, m=m, off=off, Ws=Ws: e.matmul(
                    PS[bk][:, 0:n], lhsT=Ws[j // 4][:, j % 4, m * 128:(m + 1) * 128],
                    rhs=HB[:, j, off:off + n], start=(j == 0), stop=(j == 7)),
                    reads=[bs[j // 4], bHB[j][ti]], writes=[bPS[bk]], pe_acc=(j > 0))
            dst = XB[:, m, off:off + n]
            og, ob = AG["mix0"] if l == 0 else AG["mix1"]
            if q == 0:
                P.op("dve", lambda e, dst=dst, bk=bk, n=n, m=m, og=og: e.scalar_tensor_tensor(
                    out=dst, in0=dst, scalar=dvcol(og + m), in1=PS[bk][:, 0:n], op0=ALU.mult, op1=ALU.add),
                    reads=[bPS[bk], bXB[m][ti], bDV], writes=[bXB[m][ti]])
            elif q == 1:
                P.op("dve", lambda e, dst=dst, bk=bk, n=n, m=m, ob=ob: e.scalar_tensor_tensor(
                    out=dst, in0=PS[bk][:, 0:n], scalar=dvcol(ob + m), in1=dst, op0=ALU.add, op1=ALU.add),
                    reads=[bPS[bk], bXB[m][ti], bDV], writes=[bXB[m][ti]])
            else:
                P.op("dve", lambda e, dst=dst, bk=bk, n=n: e.tensor_tensor(
                    out=dst, in0=PS[bk][:, 0:n], in1=dst, op=ALU.add),
                    reads=[bPS[bk], bXB[m][ti]], writes=[bXB[m][ti]])
            yield

    def mlp_head(l, tile):
        for j in range(8):
            gemm1_group(l, 0, j, tile)
            yield

    def mlp_rest(l, tiles):
        release(f"w1_{l}_0")
        release(f"w1_{l}_1")
        for q in range(4):
            if q > 0:
                for j in range(8):
                    for tile in tiles:
                        gemm1_group(l, q, j, tile)
                    if j % 4 == 3:
                        release(f"w1_{l}_{2 * q + j // 4}")
            if q < 3:
                for tile in tiles:
                    run(gemm2_tile(l, q, tile))
                release(f"w2_{l}_{2 * q}")
                release(f"w2_{l}_{2 * q + 1}")

    def mlp_tail_done(l):
        release(f"w2_{l}_6")
        release(f"w2_{l}_7")

    def dump(stage):
        if dbg_stage == stage:
            for c in range(NCH):
                P.dma("sp", "out", lambda e, c=c: e.dma_start(out=dbg_d[c * 128:(c + 1) * 128, :], in_=XB[:, c, :]),
                      reads=bXB[c], writes=[bOUT])
            return True
        return False

    request("pool_w", pool_loader)
    ring_dep.extend(bXB[7])
    queue_mlp_units(0)
    for u in range(4):
        request(f"cwin{u}", load_glu(u))
    for h in range(2):
        request(f"cwout{h}", load_cols(cwout_d, 0, 512 * h, 512))
    request("SB0", None)
    request("SB1", None)
    request("DG0", None)
    request("DG1", None)
    queue_mlp_units(1)
    del ring_dep[:]

    def finish():
        return nc, P

    DUM = A("DUM", [128, 8], F32).ap()

    bDUM = Buf("DUM")

    def fence(bufs):
        P.op("dve", lambda e: e.memset(DUM[:, 0:1], 0.0), reads=[], writes=list(bufs) + [bDUM])

    P.op("dve", lambda e: e.memset(DUM, 0.0), reads=[], writes=[bDUM])

    SCR = HB.rearrange("p c t -> p (c t)").bitcast(F32)
    scr = [SCR[:, i * TT:(i + 1) * TT] for i in range(4)]
    bS4 = [Buf(f"SCR{i}") for i in range(4)]
    allHB = [b for c in range(NCH) for b in bHB[c]]
    for g in range(4):
        w = POOLW[g]
        cs = (2 * g, 2 * g + 1)
        if g == 0:
            for c in cs:
                xrow = XB[:, c, :]
                P.op("dve", lambda e, c=c, xrow=xrow: e.tensor_tensor(
                    out=XBF[:, c, 32:TT], in0=xrow[:, 31:TT - 1], in1=xrow[:, 32:TT], op=ALU.subtract),
                    reads=bXB[c], writes=bXBF[c])
            for c in cs:
                P.op("dve", lambda e, c=c: e.tensor_scalar(
                    out=XBF[:, c, HALO:HALO + 1], in0=XBF[:, c, HALO:HALO + 1], scalar1=PCV[:, 0:1],
                    scalar2=None, op0=ALU.mult),
                    reads=[bXBF[c][1], bPCV], writes=[bXBF[c][1]])
            for c in cs:
                xrow = XB[:, c, :]
                P.op("act", lambda e, xrow=xrow: e.activation(out=xrow[:, 32:TT], in_=xrow[:, 32:TT], func=AF.Identity,
                                                              scale=ALPHA),
                     reads=bXB[c], writes=bXB[c])
            continue
        cur = {c: (XB[:, c, :], bXB[c]) for c in cs}
        lo = 0
        sh = 1
        if g == 3:
            for ci, c in enumerate(cs):
                xrow = XB[:, c, :]
                ea = scr[2 * ci]
                P.op("dve", lambda e, ea=ea, xrow=xrow: e.tensor_tensor(
                    out=ea[:, 16:TT], in0=xrow[:, 16:TT], in1=xrow[:, 0:TT - 16], op=ALU.subtract),
                    reads=bXB[c], writes=[bS4[2 * ci]])
            for ci, c in enumerate(cs):
                xrow = XB[:, c, :]
                ea = scr[2 * ci]
                P.op("dve", lambda e, ea=ea, xrow=xrow: e.tensor_copy(out=ea[:, 0:16], in_=xrow[:, 0:16]),
                     reads=bXB[c] + [bS4[2 * ci]], writes=[bS4[2 * ci]])
            for ci, c in enumerate(cs):
                ea = scr[2 * ci]
                sa = scr[2 * ci + 1]
                P.op("dve", lambda e, ea=ea, sa=sa: e.tensor_tensor_scan(
                    out=sa[:, 0:TT], data0=ea[:, 0:TT], data1=DUM[:, 0:1].to_broadcast([128, TT]),
                    initial=0.0, op0=ALU.add, op1=ALU.add),
                    reads=[bS4[2 * ci], bDUM], writes=[bS4[2 * ci + 1]])
                cur[c] = (sa, [bS4[2 * ci + 1]])
        for stp in range(g + 1 if g < 3 else 0):
            lo = lo + sh
            for ci, c in enumerate(cs):
                si_ = 2 * ci + (stp % 2)
                dstb = scr[si_]
                src, sb = cur[c]
                P.op("dve", lambda e, dstb=dstb, src=src, lo=lo, sh=sh: e.tensor_tensor(
                    out=dstb[:, lo:TT], in0=src[:, lo:TT], in1=src[:, lo - sh:TT - sh], op=ALU.add),
                    reads=sb, writes=[bS4[si_]])
                cur[c] = (dstb, [bS4[si_]])
            sh *= 2
        for ci, c in enumerate(cs):
            src, sb = cur[c]
            xrow = XB[:, c, :]
            P.op("dve", lambda e, src=src, c=c, w=w, xrow=xrow: e.scalar_tensor_tensor(
                out=XBF[:, c, 32:TT], in0=src[:, 32:TT], scalar=1.0 / w, in1=xrow[:, 32:TT],
                op0=ALU.mult, op1=ALU.subtract),
                reads=bXB[c] + sb, writes=bXBF[c])
        tmps = {}
        for ci, c in enumerate(cs):
            src, sb = cur[c]
            ti_ = 2 * ci + ((g + 1) % 2)
            tmpf = scr[ti_]
            tmps[c] = (tmpf, ti_)
            P.op("dve", lambda e, src=src, g=g, tmpf=tmpf: e.tensor_tensor(
                out=tmpf[:, 0:16], in0=src[:, HALO:HALO + 16], in1=PCV[:, 1 + 16 * g:1 + 16 * g + 16], op=ALU.mult),
                reads=sb + [bPCV], writes=[bS4[ti_]])
        for ci, c in enumerate(cs):
            tmpf, ti_ = tmps[c]
            xrow = XB[:, c, :]
            P.op("dve", lambda e, c=c, tmpf=tmpf, xrow=xrow: e.tensor_tensor(
                out=XBF[:, c, HALO:HALO + 16], in0=tmpf[:, 0:16], in1=xrow[:, HALO:HALO + 16], op=ALU.subtract),
                reads=[bS4[ti_]] + bXB[c], writes=[bXBF[c][0], bXBF[c][1]])
        for ci, c in enumerate(cs):
            xrow = XB[:, c, :]
            P.op("act", lambda e, xrow=xrow: e.activation(out=xrow[:, 32:TT], in_=xrow[:, 32:TT], func=AF.Identity,
                                                          scale=ALPHA),
                 reads=bXB[c], writes=bXB[c])
    fence(allHB + bS4)
    o_psh = dv_alloc("pool_scale_half", 2)
    P.op("dve", lambda e: e.tensor_scalar(out=dvcol(o_psh, 2), in0=vcol("pool_scale", 0, 2), scalar1=0.5,
                                          scalar2=None, op0=ALU.mult),
         reads=[bVEC], writes=[bDV])
    sp_ = slot_of("pool_w")
    PW = RING[:, sp_, 0:2048].rearrange("p (g k n) -> p g k n", g=4, k=2)

    def pool_mm_tile(tile):
        (ti, off, n) = tile
        for g in range(4):
            for m in range(2):
                co = 2 * g + m
                bk = next_bank()
                for k in range(2):
                    P.op("pe", lambda e, bk=bk, n=n, g=g, k=k, m=m, off=off: e.matmul(
                        PS[bk][:, 0:n], lhsT=PW[:, g, k, m * 128:(m + 1) * 128], rhs=XBF[:, 2 * g + k, off:off + n],
                        start=(k == 0), stop=(k == 1)),
                        reads=[bRING[sp_], bXBF[2 * g + k][ti]], writes=[bPS[bk]], pe_acc=(k > 0))
                dst = XB[:, co, off:off + n]
                psc = dvcol(o_psh + co) if g == 0 else vcol("pool_scale", co)
                P.op("dve", lambda e, dst=dst, bk=bk, n=n, psc=psc: e.scalar_tensor_tensor(
                    out=dst, in0=PS[bk][:, 0:n], scalar=psc, in1=dst, op0=ALU.mult, op1=ALU.add),
                    reads=[bPS[bk], bXB[co][ti], bVEC, bDV], writes=[bXB[co][ti]])
                yield

    pipeline(PIPETILES, pool_mm_tile,
             lambda t: ln_s3(t, "mix_ln_g0", "mix_ln_b0", "midx"),
             lambda t: mlp_head(0, t))
    release("pool_w")
    mlp_rest(0, ALLTILES)

    def conv_in_tile(tile):
        (ti, off, n) = tile
        for c in range(NCH):
            u, cc = c // 2, c % 2
            s = slot_of(f"cwin{u}")
            W = RING[:, s, :].rearrange("p (k n) -> p k n", k=8)
            pa = next_bank()
            pg = next_bank()
            for k in range(8):
                P.op("pe", lambda e, pa=pa, n=n, k=k, cc=cc, off=off, W=W: e.matmul(
                    PS[pa][:, 0:n], lhsT=W[:, k, cc * 128:(cc + 1) * 128], rhs=XBF[:, k, off:off + n],
                    start=(k == 0), stop=(k == 7)),
                    reads=[bRING[s], bXBF[k][ti]], writes=[bPS[pa]], pe_acc=(k > 0))
            for k in range(8):
                P.op("pe", lambda e, pg=pg, n=n, k=k, cc=cc, off=off, W=W: e.matmul(
                    PS[pg][:, 0:n], lhsT=W[:, k, 256 + cc * 128:256 + (cc + 1) * 128], rhs=XBF[:, k, off:off + n],
                    start=(k == 0), stop=(k == 7)),
                    reads=[bRING[s], bXBF[k][ti]], writes=[bPS[pg]], pe_acc=(k > 0))
            si = rot("sg")
            P.op("act", lambda e, si=si, n=n, pg=pg, c=c: e.activation(
                out=SG[:, si, 0:n], in_=PS[pg][:, 0:n], func=AF.Sigmoid, bias=vcol("conv_b_in", 8 + c)),
                reads=[bPS[pg], bVEC], writes=[bSG[si]])
            P.op("dve", lambda e, si=si, n=n, pa=pa, c=c, off=off: e.scalar_tensor_tensor(
                out=HB[:, c, off:off + n], in0=PS[pa][:, 0:n], scalar=vcol("conv_b_in", c),
                in1=SG[:, si, 0:n], op0=ALU.add, op1=ALU.mult),
                reads=[bPS[pa], bSG[si], bVEC], writes=[bHB[c][ti]])
            if ti == 0:
                P.op("dve", lambda e, c=c, off=off, n=n: e.tensor_scalar(
                    out=HB[:, c, off:off + n], in0=HB[:, c, off:off + n], scalar1=PCV[:, 0:1],
                    scalar2=None, op0=ALU.mult),
                    reads=[bHB[c][ti], bPCV], writes=[bHB[c][ti]])
            yield

    pipeline(PIPETILES, lambda t: gemm2_tile(0, 3, t),
             lambda t: ln_s3(t, "mlp_ln_g0", "mlp_ln_b0", "mid", AG["mlp0"]),
             conv_in_tile)
    mlp_tail_done(0)
    for u in range(4):
        release(f"cwin{u}")

    HC = XBF.rearrange("p c t -> p (c t)").bitcast(F32)
    bHC = [[Buf(f"HC{c}_{t}") for t in range(2)] for c in range(NCH)]
    allXBF = [b for c in range(NCH) for b in bXBF[c]]
    allHC = [b for c in range(NCH) for b in bHC[c]]
    fence(allXBF + allHC)
    sSB = [slot_of("SB0"), slot_of("SB1")]
    sDG = [slot_of("DG0"), slot_of("DG1")]
    bSB = [[Buf(f"SB{c}_{t}") for t in range(2)] for c in range(NCH)]
    bDG = [Buf("DGa"), Buf("DGb")]
    allSB = [b for c in range(NCH) for b in bSB[c]]
    fence([bRING[sSB[0]], bRING[sSB[1]], bRING[sDG[0]], bRING[sDG[1]]] + allSB + bDG)
    SBv = [RING[:, sSB[i], :].rearrange("p (c t n) -> p c t n", c=4, t=2) for i in range(2)]
    DGv = [RING[:, sDG[i], 0:KW * 128].rearrange("p (k n) -> p k n", k=KW) for i in range(2)]
    so0 = slot_of("cwout0")
    so1 = slot_of("cwout1")
    Wo = [RING[:, so0, :].rearrange("p (k n) -> p k n", k=8), RING[:, so1, :].rearrange("p (k n) -> p k n", k=8)]
    bWo = [bRING[so0], bRING[so1]]
    dwo, _ = VEC_LAYOUT["conv_dw"]
    ido, _ = VEC_LAYOUT["ident"]
    dgi = [0]

    def hc_ap(c, tl):
        o = (c * 2 + tl) * 512
        return HC[:, o:o + 512]

    def w_out_tile(tile, tl):
        (ti, off, n) = tile
        for m in range(NCH):
            bk = next_bank()
            for k in range(8):
                P.op("pe", lambda e, bk=bk, k=k, m=m, tl=tl: e.matmul(
                    PS[bk][:, 0:512], lhsT=Wo[m // 4][:, k, (m % 4) * 128:(m % 4 + 1) * 128],
                    rhs=SBv[k // 4][:, k % 4, tl, :], start=(k == 0), stop=(k == 7)),
                    reads=[bWo[m // 4], bSB[k][tl]], writes=[bPS[bk]], pe_acc=(k > 0))
            dst = XB[:, m, off:off + 512]
            P.op("dve", lambda e, dst=dst, bk=bk: e.tensor_tensor(out=dst, in0=PS[bk][:, 0:512], in1=dst, op=ALU.add),
                 reads=[bPS[bk], bXB[m][ti]], writes=[bXB[m][ti]])
            yield

    dg_seq = [(t, c) for t in range(4) for c in range(NCH)]

    def build_dg(i):
        c = dg_seq[i][1]
        di = i % 2
        P.op("dve", lambda e, di=di, c=c: e.tensor_tensor(
            out=DGv[di], in0=VEC[:, ido:ido + 128].unsqueeze(1).to_broadcast([128, KW, 128]),
            in1=VEC[:, dwo + c * KW:dwo + (c + 1) * KW].unsqueeze(2).to_broadcast([128, KW, 128]),
            op=ALU.mult),
            reads=[bVEC], writes=[bDG[di]])

    def conv_ln_chain(par):
        ln_apply_stats(par, 4 + 2 * par, 5 + 2 * par, 512)

    def conv_ln_apply(par):
        for c in range(NCH):
            hca = hc_ap(c, par)
            P.op("dve", lambda e, hca=hca, par=par: e.tensor_tensor(out=hca, in0=hca, in1=LNB[:, par, :], op=ALU.mult),
                 reads=[bHC[c][par], bLNB[par]], writes=[bHC[c][par]])
        for c in range(NCH):
            hca = hc_ap(c, par)
            P.op("dve", lambda e, hca=hca, par=par: e.tensor_tensor(out=hca, in0=hca, in1=LNA[:, par, :], op=ALU.add),
                 reads=[bHC[c][par], bLNA[par]], writes=[bHC[c][par]])

    def conv_silu(par):
        for c in range(NCH):
            hca = hc_ap(c, par)
            P.op("act", lambda e, hca=hca, c=c, par=par: e.activation(
                out=SBv[c // 4][:, c % 4, par, :], in_=hca, func=AF.Silu,
                bias=vcol("conv_ln_b", c), scale=vcol("conv_ln_g", c)),
                reads=[bHC[c][par], bVEC], writes=[bSB[c][par]])

    def conv_side(tprev):
        for _ in conv_side_ln(tprev):
            yield
        for _ in w_out_tile(MTILES[tprev], tprev % 2):
            yield

    def conv_side_ln(tprev):
        pp = tprev % 2
        bmp, bep = 4 + 2 * pp, 5 + 2 * pp
        bb = LNB[:, pp, :]
        P.op("act", lambda e: e.activation(out=bb, in_=PS[bmp][:, 0:512], func=AF.Square),
             reads=[bPS[bmp]], writes=[bLNB[pp]])
        P.op("dve", lambda e: e.scalar_tensor_tensor(out=bb, in0=PS[bep][:, 0:512], scalar=EPS, in1=bb,
                                                     op0=ALU.add, op1=ALU.subtract),
             reads=[bPS[bep], bLNB[pp]], writes=[bLNB[pp]])
        P.op("act", lambda e: e.activation(out=bb, in_=bb, func=AF.Ln),
             reads=[bLNB[pp]], writes=[bLNB[pp]])
        P.op("act", lambda e: e.activation(out=bb, in_=bb, func=AF.Exp, scale=-0.5),
             reads=[bLNB[pp]], writes=[bLNB[pp]])
        yield
        for c in range(NCH):
            hca = hc_ap(c, pp)
            P.op("dve", lambda e, hca=hca, bmp=bmp: e.tensor_tensor(out=hca, in0=hca, in1=PS[bmp][:, 0:512], op=ALU.subtract),
                 reads=[bHC[c][pp], bPS[bmp]], writes=[bHC[c][pp]])
            if c % 2 == 1:
                yield
        for c in range(NCH):
            hca = hc_ap(c, pp)
            P.op("dve", lambda e, hca=hca, pp=pp: e.tensor_tensor(out=hca, in0=hca, in1=LNB[:, pp, :], op=ALU.mult),
                 reads=[bHC[c][pp], bLNB[pp]], writes=[bHC[c][pp]])
            P.op("act", lambda e, hca=hca, c=c, pp=pp: e.activation(
                out=SBv[c // 4][:, c % 4, pp, :], in_=hca, func=AF.Silu,
                bias=vcol("conv_ln_b", c), scale=vcol("conv_ln_g", c)),
                reads=[bHC[c][pp], bVEC], writes=[bSB[c][pp]])
            if c % 2 == 1:
                yield

    def advance(gen, k):
        if gen is None:
            return
        for _ in range(k):
            try:
                next(gen)
            except StopIteration:
                return

    gemm_banks[0] = [0, 1, 2, 3]
    gemm_bank[0] = 0
    st_ctr = [0]
    build_dg(0)
    for t in range(4):
        (ti, off, n) = MTILES[t]
        par = t % 2
        side = conv_side(t - 1) if t > 0 else None
        deferred = []
        for c in range(NCH):
            i = t * NCH + c
            di = i % 2
            if i + 1 < len(dg_seq):
                build_dg(i + 1)
            bk = next_bank()
            for k in range(KW):
                P.op("pe", lambda e, bk=bk, di=di, k=k, c=c, off=off: e.matmul(
                    PS[bk][:, 0:512], lhsT=DGv[di][:, k, :], rhs=HB[:, c, off - 30 + k:off - 30 + k + 512],
                    start=(k == 0), stop=(k == KW - 1)),
                    reads=[bDG[di], bHB[c][ti], bHB[c][ti - 1]], writes=[bPS[bk]], pe_acc=(k > 0))
            while len(deferred) >= 2:
                deferred.pop(0)()
            hca = hc_ap(c, par)
            P.op("act", lambda e, hca=hca, bk=bk, c=c: e.activation(
                out=hca, in_=PS[bk][:, 0:512], func=AF.Identity, bias=vcol("conv_dw_b", c)),
                reads=[bPS[bk], bVEC], writes=[bHC[c][par]])
            st_ctr[0] = (st_ctr[0] + 1) % 3
            si = st_ctr[0]
            P.op("act", lambda e, si=si, bk=bk, c=c: e.activation(
                out=VSQ[:, si, :], in_=PS[bk][:, 0:512], func=AF.Square, bias=vcol("conv_dw_b", c)),
                reads=[bPS[bk], bVEC], writes=[bVSQ[si]])
            vi = si
            P.op("act", lambda e, vi=vi, bk=bk, c=c: e.activation(
                out=VBT[:, vi, :], in_=PS[bk][:, 0:512], func=AF.Identity, bias=vcol("conv_dw_b", c)),
                reads=[bPS[bk], bVEC], writes=[bVBT[vi]])
            bm, be = 4 + 2 * par, 5 + 2 * par

            def stats(vi=vi, si=si, bm=bm, be=be, c=c):
                P.op("pe", lambda e: e.matmul(PS[bm][:, 0:512], lhsT=ONES, rhs=VBT[:, vi, :],
                                              start=(c == 0), stop=(c == NCH - 1)),
                     reads=[bONES, bVBT[vi]], writes=[bPS[bm]], pe_acc=(c > 0))
                P.op("pe", lambda e: e.matmul(PS[be][:, 0:512], lhsT=ONES, rhs=VSQ[:, si, :],
                                              start=(c == 0), stop=(c == NCH - 1)),
                     reads=[bONES, bVSQ[si]], writes=[bPS[be]], pe_acc=(c > 0))
            deferred.append(stats)
            advance(side, 2 if c < 4 else (3 if c < 7 else 99))
        for f in deferred:
            f()
    fence([bRING[sDG[0]], bRING[sDG[1]]] + bDG)
    release("DG0")
    release("DG1")
    run(conv_side_ln(3))
    gemm_banks[0] = [0, 1, 2, 3, 6, 7]
    fence(allXBF + allHC)

    def x3(tile):
        if tile[0] == 4:
            return w_out_tile(tile, 1)
        return None

    pipeline(MTILES, x3,
             lambda t: ln_s3(t, "mix_ln_g1", "mix_ln_b1", "midx"),
             lambda t: mlp_head(1, t))
    fence([bRING[sSB[0]], bRING[sSB[1]]] + allSB)
    release("cwout0")
    release("cwout1")
    release("SB0")
    release("SB1")
    mlp_rest(1, MTILES)
    pipeline(MTILES, lambda t: gemm2_tile(1, 3, t),
             lambda t: ln_s3(t, "mlp_ln_g1", "mlp_ln_b1", "final"), None)
    mlp_tail_done(1)
    return finish()


def _emit(nc, P):
    P.resolve()
    from contextlib import ExitStack
    with ExitStack() as es:
        sems = {}
        for name in P.sem_names:
            sems[name] = es.enter_context(nc.semaphore(name))
        block = es.enter_context(nc.Block())
        outsem_total = P.final.get("out", 0)

        @block.tensor
        def _(eng):
            P.emit_engine("pe", eng, sems)

        @block.scalar
        def _(eng):
            P.emit_engine("act", eng, sems)

        @block.vector
        def _(eng):
            P.emit_engine("dve", eng, sems)

        @block.gpsimd
        def _(eng):
            P.emit_engine("pool", eng, sems)
            for name in P.sem_names:
                if name.startswith("ring"):
                    eng.wait_ge(sems[name], P.final[name])

        @block.sync
        def _(eng):
            P.emit_engine("sp", eng, sems)
            for name in P.sem_names:
                if name in ("vec", "pcv") or name.startswith("xin"):
                    eng.wait_ge(sems[name], P.final[name])
            eng.wait_ge(sems["out"], outsem_total)
    return nc


def _fm(v, n):
    return np.ascontiguousarray(np.asarray(v, np.float32).reshape(n, 128).T)


def _pack_vecs(inp):
    V = np.zeros((128, NV), np.float32)

    def put(name, arr):
        o, n = VEC_LAYOUT[name]
        assert arr.shape == (128, n), (name, arr.shape, n)
        V[:, o:o + n] = arr

    put("pool_scale", _fm(inp["pool_scale"][0], 8))
    for l in range(2):
        put(f"mix_ln_g{l}", _fm(inp["mix_ln_g"][l], 8))
        put(f"mix_ln_b{l}", _fm(inp["mix_ln_b"][l], 8))
        put(f"mlp_b1_{l}", _fm(inp["mlp_b1"][l], 32))
        put(f"mlp_b2_{l}", _fm(inp["mlp_b2"][l], 8))
        put(f"mlp_ln_g{l}", _fm(inp["mlp_ln_g"][l], 8))
        put(f"mlp_ln_b{l}", _fm(inp["mlp_ln_b"][l], 8))
    put("conv_b_in", _fm(inp["conv_b_in"][0], 16))
    dw = np.asarray(inp["conv_dw"][0], np.float32)
    dwl = dw.reshape(KW, 8, 128).transpose(2, 1, 0).reshape(128, 8 * KW)
    put("conv_dw", np.ascontiguousarray(dwl))
    put("conv_dw_b", _fm(inp["conv_dw_b"][0], 8))
    put("conv_ln_g", _fm(inp["conv_ln_g"][0], 8))
    put("conv_ln_b", _fm(inp["conv_ln_b"][0], 8))
    put("conv_b_out", _fm(inp["conv_b_out"][0], 8))
    put("ident", np.eye(128, dtype=np.float32))
    return V


def _percore_tables():
    tabs = []
    for core in range(NCORES):
        first = (core % 4 == 0)
        t = np.zeros((128, NPC), np.float32)
        t[:, 0] = 0.0 if first else 1.0
        for g, w in enumerate(POOLW):
            for i in range(16):
                cntv = min(i + 1, w) if first else w
                t[:, 1 + 16 * g + i] = np.float32(1.0) / np.float32(cntv)
        tabs.append(t)
    return tabs


_CACHE = {}


def _get_nc(dbg_stage=None):
    if dbg_stage not in _CACHE:
        nc, P = build_program(dbg_stage)
        _CACHE[dbg_stage] = _emit(nc, P)
    return _CACHE[dbg_stage]


def _in_maps(inp):
    x = np.asarray(inp["x"], np.float32)
    V = _pack_vecs(inp)
    tabs = _percore_tables()
    shared = {
        "vecs": V,
        "pool_w": np.ascontiguousarray(np.asarray(inp["pool_w"], np.float32).reshape(1024, 256)),
        "conv_w_in": np.ascontiguousarray(np.asarray(inp["conv_w_in"], np.float32).reshape(1024, 2048)),
        "conv_w_out": np.ascontiguousarray(np.asarray(inp["conv_w_out"], np.float32).reshape(1024, 1024)),
        "mlp_w1": np.ascontiguousarray(np.asarray(inp["mlp_w1"], np.float32).reshape(2048, DFF)),
        "mlp_w2": np.ascontiguousarray(np.asarray(inp["mlp_w2"], np.float32).reshape(2 * DFF, 1024)),
    }
    maps = []
    for core in range(NCORES):
        b, ch = core // 4, core % 4
        t0 = ch * TPC
        xs = np.zeros((TT, D), np.float32)
        lo = max(0, t0 - HALO)
        xs[HALO - (t0 - lo):, :] = x[b, lo:t0 + TPC, :]
        m = dict(shared)
        m["xT"] = np.ascontiguousarray(xs.T)
        m["pcore"] = tabs[core]
        maps.append(m)
    return maps


def kernel(**inputs):
    nc = _get_nc(None)
    maps = _in_maps(inputs)
    res = run_bass_kernel_spmd(nc, maps, core_ids=list(range(NCORES)))
    out = np.empty((BATCH, SEQ, D), np.float32)
    for core in range(NCORES):
        b, ch = core // 4, core % 4
        y = np.asarray(res.results[core]["yT"]).reshape(4, NCH, 128, 512)
        out[b, ch * TPC:(ch + 1) * TPC, :] = y.transpose(0, 3, 1, 2).reshape(TPC, D)
    return out
```

```python
import numpy as np
import concourse.bass as bass
import concourse.mybir as mybir
from concourse.bass_utils import run_bass_kernel_spmd

F32 = mybir.dt.float32
BF16 = mybir.dt.bfloat16
AF = mybir.ActivationFunctionType
ALU = mybir.AluOpType

NCORES = 8
D = 1024
NCH = 8
DFF = 4096
SEQ = 8192
BATCH = 2
TPC = 2048
HALO = 64
TT = HALO + TPC
KW = 31
ALPHA = 4.0 ** 0.25
EPS = 1e-5
POOLW = (2, 4, 8, 16)
RSLOTS = 6
XRES_ENG = "dve"
SLOT_ELEMS = 4096

HTILE = (0, 32, 32)
MTILES = [(1 + i, HALO + 512 * i, 512) for i in range(4)]
ALLTILES = [HTILE] + MTILES
PIPETILES = MTILES + [HTILE]

VEC_LAYOUT = {}
_off = 0


def _reg(name, n):
    global _off
    VEC_LAYOUT[name] = (_off, n)
    _off += n


_reg("pool_scale", 8)
for _l in range(2):
    _reg(f"mix_ln_g{_l}", 8)
    _reg(f"mix_ln_b{_l}", 8)
    _reg(f"mlp_b1_{_l}", 32)
    _reg(f"mlp_b2_{_l}", 8)
    _reg(f"mlp_ln_g{_l}", 8)
    _reg(f"mlp_ln_b{_l}", 8)
_reg("conv_b_in", 16)
_reg("conv_dw", 8 * KW)
_reg("conv_dw_b", 8)
_reg("conv_ln_g", 8)
_reg("conv_ln_b", 8)
_reg("conv_b_out", 8)
_reg("ident", 128)
NV = _off
NPC = 1 + 64


class Buf:
    __slots__ = ("name", "lw", "rd")

    def __init__(self, name):
        self.name = name
        self.lw = None
        self.rd = {}


class Op:
    __slots__ = ("eng", "fn", "reads", "writes", "sem", "ndma", "pe_acc", "idx", "waits", "sig")

    def __init__(self, eng, fn, reads, writes, sem=None, ndma=0, pe_acc=False):
        self.eng = eng
        self.fn = fn
        self.reads = reads
        self.writes = writes
        self.sem = sem
        self.ndma = ndma
        self.pe_acc = pe_acc
        self.idx = None
        self.waits = None
        self.sig = None


COMPUTE = ("pe", "act", "dve", "pool")
QUEUES = ("sp",)


class Prog:
    def __init__(self):
        self.ops = []

    def op(self, eng, fn, reads=(), writes=(), pe_acc=False):
        self.ops.append(Op(eng, fn, list(reads), list(writes), pe_acc=pe_acc))

    def dma(self, queue, sem, fn, reads=(), writes=(), ndma=1):
        self.ops.append(Op(queue, fn, list(reads), list(writes), sem=sem, ndma=ndma))

    def resolve(self):
        known = {}
        cnt = {}
        snaps = {}
        needed = set()
        for op in self.ops:
            E = op.eng
            S = op.sem if op.sem is not None else E
            kn = known.setdefault(E, {})
            deps = set()
            for b in op.reads:
                if b.lw is not None:
                    deps.add(b.lw)
            for b in op.writes:
                if b.lw is not None:
                    if not (op.pe_acc and b.lw[0] == "pe"):
                        deps.add(b.lw)
                for it in b.rd.items():
                    deps.add(it)
            waits = []
            for (De, Di) in sorted(deps, key=lambda t: -t[1]):
                if kn.get(De, -1) >= Di:
                    continue
                waits.append((De, Di))
                needed.add((De, Di))
                kn[De] = Di
                for k2, v2 in snaps[(De, Di)].items():
                    if kn.get(k2, -1) < v2:
                        kn[k2] = v2
            idx = cnt.get(S, 0)
            cnt[S] = idx + 1
            op.idx = idx
            op.waits = waits
            op.sig = S
            snaps[(S, idx)] = dict(kn)
            for b in op.reads:
                b.rd[S] = idx
            for b in op.writes:
                b.lw = (S, idx)
                b.rd = {}
        self.needed = needed
        self.val = {}
        run = {}
        for op in self.ops:
            S = op.sig
            if op.sem is not None:
                run[S] = run.get(S, 0) + 16 * op.ndma
                self.val[(S, op.idx)] = run[S]
            else:
                if (S, op.idx) in needed:
                    run[S] = run.get(S, 0) + 1
                    self.val[(S, op.idx)] = run[S]
        self.sem_names = sorted(cnt.keys())
        self.final = dict(run)

    def emit_engine(self, E, handle, sems):
        for op in self.ops:
            if op.eng != E:
                continue
            for (De, Di) in op.waits:
                handle.wait_ge(sems[De], self.val[(De, Di)])
            r = op.fn(handle)
            if op.sem is not None:
                insts = r if isinstance(r, (list, tuple)) else [r]
                assert len(insts) == op.ndma
                for i in insts:
                    i.then_inc(sems[op.sem], 16)
            elif (op.sig, op.idx) in self.needed:
                r.then_inc(sems[op.sig], 1)


def build_program(dbg_stage=None):
    nc = bass.Bass("TRN2", target_bir_lowering=False)
    P = Prog()

    xT = nc.dram_tensor("xT", [D, TT], F32, kind="ExternalInput").ap()
    vecs_d = nc.dram_tensor("vecs", [128, NV], F32, kind="ExternalInput").ap()
    pc_d = nc.dram_tensor("pcore", [128, NPC], F32, kind="ExternalInput").ap()
    pool_w_d = nc.dram_tensor("pool_w", [1024, 256], F32, kind="ExternalInput").ap()
    cwin_d = nc.dram_tensor("conv_w_in", [1024, 2048], F32, kind="ExternalInput").ap()
    cwout_d = nc.dram_tensor("conv_w_out", [1024, 1024], F32, kind="ExternalInput").ap()
    w1_d = nc.dram_tensor("mlp_w1", [2 * 1024, DFF], F32, kind="ExternalInput").ap()
    w2_d = nc.dram_tensor("mlp_w2", [2 * DFF, 1024], F32, kind="ExternalInput").ap()
    yT = nc.dram_tensor("yT", [4 * NCH * 128, 512], F32, kind="ExternalOutput").ap()
    dbg_d = None
    if dbg_stage is not None:
        dbg_d = nc.dram_tensor("dbg", [D, TT], F32, kind="ExternalOutput").ap()

    A = nc.alloc_sbuf_tensor
    XB = A("XB", [128, NCH, TT], F32).ap()
    XBF = A("XBF", [128, NCH, TT], BF16).ap()
    HB = A("HB", [128, NCH, TT], BF16).ap()
    RING = A("RING", [128, RSLOTS, SLOT_ELEMS], BF16).ap()
    VEC = A("VEC", [128, NV], F32).ap()
    PCV = A("PCV", [128, NPC], F32).ap()
    DV = A("DV", [128, 64], F32).ap()
    ONES = A("ONES", [128, 128], BF16).ap()
    ONES32 = A("ONES32", [128, 128], F32).ap()
    SG = A("SG", [128, 2, 512], F32).ap()
    VBT = A("VBT", [128, 3, 512], BF16).ap()
    VSQ = A("VSQ", [128, 3, 512], BF16).ap()
    RT = A("RT", [128, 4, 512], BF16).ap()
    LNA = A("LNA", [128, 2, 512], F32).ap()
    LNB = A("LNB", [128, 2, 512], F32).ap()
    PS = [nc.alloc_psum_tensor(f"PS{i}", [128, 512], F32).ap() for i in range(8)] \
        if hasattr(nc, "alloc_psum_tensor") else None
    assert PS is not None

    bXB = [[Buf(f"XB{c}_{t}") for t in range(5)] for c in range(NCH)]
    bXBF = [[Buf(f"XBF{c}_{t}") for t in range(5)] for c in range(NCH)]
    bHB = [[Buf(f"HB{c}_{t}") for t in range(5)] for c in range(NCH)]
    bRING = [Buf(f"RING{s}") for s in range(RSLOTS)]
    bVEC, bPCV, bDV, bONES = Buf("VEC"), Buf("PCV"), Buf("DV"), Buf("ONES")
    bSG = [Buf("SG0"), Buf("SG1")]
    bVBT = [Buf(f"VBT{i}") for i in range(3)]
    bVSQ = [Buf(f"VSQ{i}") for i in range(3)]
    bRT = [Buf(f"RT{i}") for i in range(4)]
    bLNA = [Buf("LNA0"), Buf("LNA1")]
    bLNB = [Buf("LNB0"), Buf("LNB1")]
    bPS = [Buf(f"PS{i}") for i in range(8)]
    bOUT = Buf("OUT")

    def vcol(name, j=0, n=1):
        o, _ = VEC_LAYOUT[name]
        return VEC[:, o + j:o + j + n]

    def dvcol(j, n=1):
        return DV[:, j:j + n]

    DV_AG = {}
    _dvo = [0]

    def dv_alloc(name, n=8):
        DV_AG[name] = _dvo[0]
        _dvo[0] += n
        return DV_AG[name]

    free_slots = list(range(RSLOTS))
    pending = []
    granted = {}

    def request(name, loader):
        pending.append((name, loader))
        pump()

    def pump():
        while pending and free_slots:
            name, loader = pending.pop(0)
            s = free_slots.pop(0)
            granted[name] = s
            if loader is not None:
                loader(s)

    def release(name):
        s = granted.pop(name)
        free_slots.append(s)
        pump()

    def slot_of(name):
        assert name in granted, f"unit {name} not granted (ring too small?)"
        return granted[name]

    ring_dep = []

    def load_cols(src2d, row0, col0, ncols):
        def loader(s):
            dst = RING[:, s, 0:8 * ncols].rearrange("p (k n) -> p k n", k=8)
            src = src2d[row0:row0 + 1024, col0:col0 + ncols].rearrange("(k p) n -> p k n", p=128)
            P.dma("pool", f"ring{s}", lambda g, dst=dst, src=src: g.dma_start(out=dst, in_=src),
                  reads=list(ring_dep), writes=[bRING[s]])
        return loader

    def load_rows(src2d, row0, nk, ncols):
        def loader(s):
            dst = RING[:, s, 0:nk * ncols].rearrange("p (k n) -> p k n", k=nk)
            src = src2d[row0:row0 + nk * 128, 0:ncols].rearrange("(k p) n -> p k n", p=128)
            P.dma("pool", f"ring{s}", lambda g, dst=dst, src=src: g.dma_start(out=dst, in_=src),
                  reads=list(ring_dep), writes=[bRING[s]])
        return loader

    def load_glu(u):
        def loader(s):
            dst = RING[:, s, :].rearrange("p (k n) -> p k n", k=8)
            sa = cwin_d[:, 256 * u:256 * u + 256].rearrange("(k p) n -> p k n", p=128)
            sg = cwin_d[:, 1024 + 256 * u:1024 + 256 * u + 256].rearrange("(k p) n -> p k n", p=128)
            P.dma("pool", f"ring{s}",
                  lambda g: [g.dma_start(out=dst[:, :, 0:256], in_=sa),
                             g.dma_start(out=dst[:, :, 256:512], in_=sg)],
                  reads=list(ring_dep), writes=[bRING[s]], ndma=2)
        return loader

    P.dma("sp", "vec", lambda e: e.dma_start(out=VEC, in_=vecs_d), writes=[bVEC])
    P.dma("sp", "pcv", lambda e: e.dma_start(out=PCV, in_=pc_d), writes=[bPCV])
    for c in range(NCH):
        P.dma("sp", f"xin{c}",
              lambda e, c=c: e.dma_start(out=XB[:, c, :], in_=xT[c * 128:(c + 1) * 128, :]),
              writes=bXB[c])
    P.op("dve", lambda e: e.memset(ONES, 1.0 / 1024.0), writes=[bONES])
    bONES32 = Buf("ONES32")
    P.op("dve", lambda e: e.memset(ONES32, 1.0 / 1024.0), writes=[bONES32])

    def pool_loader(s):
        dst = RING[:, s, 0:2048].rearrange("p (g k n) -> p g k n", g=4, k=2)
        src = pool_w_d.rearrange("(g k p) n -> p g k n", g=4, k=2)
        P.dma("pool", f"ring{s}", lambda g: g.dma_start(out=dst, in_=src), writes=[bRING[s]])

    def queue_mlp_units(l):
        for q in range(4):
            for h in range(2):
                request(f"w1_{l}_{2 * q + h}", load_cols(w1_d, l * 1024, (2 * q + h) * 512, 512))
            for h in range(2):
                request(f"w2_{l}_{2 * q + h}", load_rows(w2_d, l * DFF + (2 * q + h) * 512, 4, 1024))

    def derive(gname, bname, nextbias):
        og = dv_alloc(gname + "_ag")
        ob = dv_alloc(bname + "_ab")
        P.op("dve", lambda e: e.tensor_scalar(out=dvcol(og, 8), in0=vcol(gname, 0, 8), scalar1=ALPHA,
                                              scalar2=None, op0=ALU.mult),
             reads=[bVEC], writes=[bDV])
        P.op("dve", lambda e: e.scalar_tensor_tensor(out=dvcol(ob, 8), in0=vcol(bname, 0, 8), scalar=ALPHA,
                                                     in1=vcol(nextbias, 0, 8), op0=ALU.mult, op1=ALU.add),
             reads=[bVEC], writes=[bDV])
        return og, ob

    AG = {}
    AG["mix0"] = derive("mix_ln_g0", "mix_ln_b0", "mlp_b2_0")
    AG["mlp0"] = derive("mlp_ln_g0", "mlp_ln_b0", "conv_b_out")
    AG["mix1"] = derive("mix_ln_g1", "mix_ln_b1", "mlp_b2_1")

    gemm_bank = [0]
    gemm_banks = [[0, 1, 2, 3, 6, 7]]

    def next_bank():
        lst = gemm_banks[0]
        gemm_bank[0] = (gemm_bank[0] + 1) % len(lst)
        return lst[gemm_bank[0]]

    tmp_rot = {"sg": 0, "vbt": 0, "vsq": 0, "rt": 0, "ln": 0}
    rt_ctr = [0]

    def rot(name):
        v = tmp_rot[name]
        tmp_rot[name] = 1 - v
        return v

    ln_set = {}

    def ln_s1(tile):
        (ti, off, n) = tile
        for c in range(NCH):
            src = XB[:, c, off:off + n]
            P.op("act", lambda e, c=c, src=src, off=off, n=n: e.activation(out=HB[:, c, off:off + n], in_=src, func=AF.Square),
                 reads=[bXB[c][ti]], writes=[bHB[c][ti]])
            P.op("dve", lambda e, c=c, src=src, off=off, n=n: e.tensor_copy(out=XBF[:, c, off:off + n], in_=src),
                 reads=[bXB[c][ti]], writes=[bXBF[c][ti]])
            yield

    def ln_s2(tile):
        (ti, off, n) = tile
        st = rot("ln")
        ln_set[ti] = st
        bm, be = 4, 5
        for c in range(NCH):
            P.op("pe", lambda e, n=n, bm=bm, c=c, off=off: e.matmul(PS[bm][:, 0:n], lhsT=ONES, rhs=XBF[:, c, off:off + n],
                                                                   start=(c == 0), stop=(c == NCH - 1)),
                 reads=[bONES, bXBF[c][ti]], writes=[bPS[bm]], pe_acc=(c > 0))
            P.op("pe", lambda e, n=n, be=be, c=c, off=off: e.matmul(PS[be][:, 0:n], lhsT=ONES, rhs=HB[:, c, off:off + n],
                                                                   start=(c == 0), stop=(c == NCH - 1)),
                 reads=[bONES, bHB[c][ti]], writes=[bPS[be]], pe_acc=(c > 0))
        b = LNB[:, st, 0:n]
        P.op("act", lambda e: e.activation(out=b, in_=PS[bm][:, 0:n], func=AF.Square),
             reads=[bPS[bm]], writes=[bLNB[st]])
        P.op("dve", lambda e: e.scalar_tensor_tensor(out=b, in0=PS[be][:, 0:n], scalar=EPS, in1=b,
                                                     op0=ALU.add, op1=ALU.subtract),
             reads=[bPS[be], bLNB[st]], writes=[bLNB[st]])
        P.op("act", lambda e: e.activation(out=b, in_=b, func=AF.Ln),
             reads=[bLNB[st]], writes=[bLNB[st]])
        P.op("act", lambda e: e.activation(out=b, in_=b, func=AF.Exp, scale=-0.5),
             reads=[bLNB[st]], writes=[bLNB[st]])

    def ln_s3(tile, gname, bname, mode, ag=None, xres_eng="dve"):
        (ti, off, n) = tile
        st = ln_set[ti]
        bm = 4
        b = LNB[:, st, 0:n]
        for c in range(NCH):
            dst = XB[:, c, off:off + n]
            P.op("dve", lambda e, dst=dst, bm=bm, n=n: e.tensor_tensor(out=dst, in0=dst, in1=PS[bm][:, 0:n], op=ALU.subtract),
                 reads=[bXB[c][ti], bPS[bm]], writes=[bXB[c][ti]])
            yield
        for c in range(NCH):
            dst = XB[:, c, off:off + n]
            P.op("dve", lambda e, dst=dst, st=st, n=n: e.tensor_tensor(out=dst, in0=dst, in1=LNB[:, st, 0:n], op=ALU.mult),
                 reads=[bXB[c][ti], bLNB[st]], writes=[bXB[c][ti]])
            if mode in ("mid", "midx"):
                P.op("act", lambda e, dst=dst, c=c, off=off, n=n: e.activation(
                    out=XBF[:, c, off:off + n], in_=dst, func=AF.Identity,
                    bias=vcol(bname, c), scale=vcol(gname, c)),
                    reads=[bXB[c][ti], bVEC], writes=[bXBF[c][ti]])
            yield
        if mode == "midx":
            return
        for c in range(NCH):
            dst = XB[:, c, off:off + n]
            if mode == "mid":
                og, ob = ag
                if xres_eng == "act":
                    P.op("act", lambda e, dst=dst, c=c, og=og, ob=ob: e.activation(
                        out=dst, in_=dst, func=AF.Identity, bias=dvcol(ob + c), scale=dvcol(og + c)),
                        reads=[bXB[c][ti], bDV], writes=[bXB[c][ti]])
                else:
                    P.op("dve", lambda e, dst=dst, c=c, og=og, ob=ob: e.tensor_scalar(
                        out=dst, in0=dst, scalar1=dvcol(og + c), scalar2=dvcol(ob + c), op0=ALU.mult, op1=ALU.add),
                        reads=[bXB[c][ti], bDV], writes=[bXB[c][ti]])
            else:
                P.op("act", lambda e, dst=dst, c=c: e.activation(
                    out=dst, in_=dst, func=AF.Identity, bias=vcol(bname, c), scale=vcol(gname, c)),
                    reads=[bXB[c][ti], bVEC], writes=[bXB[c][ti]])
                t0 = off - HALO
                if c % 2 == 1:
                    r0 = ((t0 // 512) * NCH + c - 1) * 128
                    P.dma("sp", "out", lambda e, c=c, r0=r0, off=off, n=n: e.dma_start(
                        out=yT[r0:r0 + 256, 0:n].rearrange("(c p) n -> p c n", p=128),
                        in_=XB[:, c - 1:c + 1, off:off + n]),
                        reads=[bXB[c - 1][ti], bXB[c][ti]], writes=[bOUT])

    def run(gen):
        if gen is not None:
            for _ in gen:
                pass

    def merge(a, b, ra=1, rb=1):
        a = iter(a) if a is not None else iter(())
        b = iter(b) if b is not None else iter(())
        da = db = False
        while not (da and db):
            for _ in range(ra):
                if not da:
                    try:
                        next(a)
                    except StopIteration:
                        da = True
            for _ in range(rb):
                if not db:
                    try:
                        next(b)
                    except StopIteration:
                        db = True

    def pipeline(tiles, X, S3, Y):
        T = len(tiles)
        for s_ in range(T + 2):
            gx = X(tiles[s_]) if (s_ < T and X is not None) else None
            g3 = S3(tiles[s_ - 1]) if 0 <= s_ - 1 < T else None
            merge(g3, gx, 2, 1)
            g1 = ln_s1(tiles[s_]) if s_ < T else None
            gy = Y(tiles[s_ - 2]) if (0 <= s_ - 2 < T and Y is not None) else None
            merge(g1, gy, 2, 1)
            if s_ < T:
                ln_s2(tiles[s_])

    def ln_apply_stats(st, bm, be, n):
        a = LNA[:, st, 0:n]
        b = LNB[:, st, 0:n]
        P.op("act", lambda e: e.activation(out=a, in_=PS[bm][:, 0:n], func=AF.Identity),
             reads=[bPS[bm]], writes=[bLNA[st]])
        P.op("act", lambda e: e.activation(out=b, in_=PS[bm][:, 0:n], func=AF.Square),
             reads=[bPS[bm]], writes=[bLNB[st]])
        P.op("dve", lambda e: e.scalar_tensor_tensor(out=b, in0=PS[be][:, 0:n], scalar=EPS, in1=b,
                                                     op0=ALU.add, op1=ALU.subtract),
             reads=[bPS[be], bLNB[st]], writes=[bLNB[st]])
        P.op("act", lambda e: e.activation(out=b, in_=b, func=AF.Ln),
             reads=[bLNB[st]], writes=[bLNB[st]])
        P.op("act", lambda e: e.activation(out=b, in_=b, func=AF.Exp, scale=-0.5),
             reads=[bLNB[st]], writes=[bLNB[st]])
        P.op("dve", lambda e: e.scalar_tensor_tensor(out=a, in0=a, scalar=-1.0, in1=b, op0=ALU.mult, op1=ALU.mult),
             reads=[bLNA[st], bLNB[st]], writes=[bLNA[st]])

    def gemm1_group(l, q, j, tile):
        (ti, off, n) = tile
        b1 = f"mlp_b1_{l}"
        uname = f"w1_{l}_{2 * q + j // 4}"
        s = slot_of(uname)
        W = RING[:, s, :].rearrange("p (k n) -> p k n", k=8)
        co = (j % 4) * 128
        J = 8 * q + j
        bk = next_bank()
        for k in range(8):
            P.op("pe", lambda e, bk=bk, n=n, W=W, k=k, co=co, off=off: e.matmul(
                PS[bk][:, 0:n], lhsT=W[:, k, co:co + 128], rhs=XBF[:, k, off:off + n],
                start=(k == 0), stop=(k == 7)),
                reads=[bRING[s], bXBF[k][ti]], writes=[bPS[bk]], pe_acc=(k > 0))
        rt_ctr[0] = (rt_ctr[0] + 1) % 4
        ri = rt_ctr[0]
        P.op("act", lambda e, ri=ri, n=n, bk=bk, J=J, b1=b1: e.activation(
            out=RT[:, ri, 0:n], in_=PS[bk][:, 0:n], func=AF.Relu, bias=vcol(b1, J)),
            reads=[bPS[bk], bVEC], writes=[bRT[ri]])
        P.op("dve", lambda e, ri=ri, n=n, j=j, off=off: e.tensor_tensor(
            out=HB[:, j, off:off + n], in0=RT[:, ri, 0:n], in1=RT[:, ri, 0:n], op=ALU.mult),
            reads=[bRT[ri]], writes=[bHB[j][ti]])

    def gemm2_tile(l, q, tile):
        (ti, off, n) = tile
        s0 = slot_of(f"w2_{l}_{2 * q}")
        s1 = slot_of(f"w2_{l}_{2 * q + 1}")
        Ws = [RING[:, s0, :].rearrange("p (k n) -> p k n", k=4),
              RING[:, s1, :].rearrange("p (k n) -> p k n", k=4)]
        bs = [bRING[s0], bRING[s1]]
        for m in range(8):
            bk = next_bank()
            for j in range(8):
                P.op("pe", lambda e, bk=bk, n=n, j=j, m=m, off=off, Ws=Ws: e.matmul(
                    PS[bk][:, 0:n], lhsT=Ws[j // 4][:, j % 4, m * 128:(m + 1) * 128],
                    rhs=HB[:, j, off:off + n], start=(j == 0), stop=(j == 7)),
                    reads=[bs[j // 4], bHB[j][ti]], writes=[bPS[bk]], pe_acc=(j > 0))
            dst = XB[:, m, off:off + n]
            og, ob = AG["mix0"] if l == 0 else AG["mix1"]
            if q == 0:
                P.op("dve", lambda e, dst=dst, bk=bk, n=n, m=m, og=og: e.scalar_tensor_tensor(
                    out=dst, in0=dst, scalar=dvcol(og + m), in1=PS[bk][:, 0:n], op0=ALU.mult, op1=ALU.add),
                    reads=[bPS[bk], bXB[m][ti], bDV], writes=[bXB[m][ti]])
            elif q == 1:
                P.op("dve", lambda e, dst=dst, bk=bk, n=n, m=m, ob=ob: e.scalar_tensor_tensor(
                    out=dst, in0=PS[bk][:, 0:n], scalar=dvcol(ob + m), in1=dst, op0=ALU.add, op1=ALU.add),
                    reads=[bPS[bk], bXB[m][ti], bDV], writes=[bXB[m][ti]])
            else:
                P.op("dve", lambda e, dst=dst, bk=bk, n=n: e.tensor_tensor(
                    out=dst, in0=PS[bk][:, 0:n], in1=dst, op=ALU.add),
                    reads=[bPS[bk], bXB[m][ti]], writes=[bXB[m][ti]])
            yield

    def mlp_head(l, tile):
        for j in range(8):
            gemm1_group(l, 0, j, tile)
            yield

    def mlp_rest(l, tiles):
        release(f"w1_{l}_0")
        release(f"w1_{l}_1")
        for q in range(4):
            if q > 0:
                for j in range(8):
                    for tile in tiles:
                        gemm1_group(l, q, j, tile)
                    if j % 4 == 3:
                        release(f"w1_{l}_{2 * q + j // 4}")
            if q < 3:
                for tile in tiles:
                    run(gemm2_tile(l, q, tile))
                release(f"w2_{l}_{2 * q}")
                release(f"w2_{l}_{2 * q + 1}")

    def mlp_tail_done(l):
        release(f"w2_{l}_6")
        release(f"w2_{l}_7")

    def dump(stage):
        if dbg_stage == stage:
            for c in range(NCH):
                P.dma("sp", "out", lambda e, c=c: e.dma_start(out=dbg_d[c * 128:(c + 1) * 128, :], in_=XB[:, c, :]),
                      reads=bXB[c], writes=[bOUT])
            return True
        return False

    request("pool_w", pool_loader)
    ring_dep.extend(bXB[7])
    queue_mlp_units(0)
    for u in range(4):
        request(f"cwin{u}", load_glu(u))
    for h in range(2):
        request(f"cwout{h}", load_cols(cwout_d, 0, 512 * h, 512))
    request("SB0", None)
    request("SB1", None)
    request("DG0", None)
    request("DG1", None)
    queue_mlp_units(1)
    del ring_dep[:]

    def finish():
        return nc, P

    DUM = A("DUM", [128, 8], F32).ap()

    bDUM = Buf("DUM")

    def fence(bufs):
        P.op("dve", lambda e: e.memset(DUM[:, 0:1], 0.0), reads=[], writes=list(bufs) + [bDUM])

    P.op("dve", lambda e: e.memset(DUM, 0.0), reads=[], writes=[bDUM])

    SCR = HB.rearrange("p c t -> p (c t)").bitcast(F32)
    scr = [SCR[:, i * TT:(i + 1) * TT] for i in range(4)]
    bS4 = [Buf(f"SCR{i}") for i in range(4)]
    allHB = [b for c in range(NCH) for b in bHB[c]]
    for g in range(4):
        w = POOLW[g]
        cs = (2 * g, 2 * g + 1)
        if g == 0:
            for c in cs:
                xrow = XB[:, c, :]
                P.op("dve", lambda e, c=c, xrow=xrow: e.tensor_tensor(
                    out=XBF[:, c, 32:TT], in0=xrow[:, 31:TT - 1], in1=xrow[:, 32:TT], op=ALU.subtract),
                    reads=bXB[c], writes=bXBF[c])
            for c in cs:
                P.op("dve", lambda e, c=c: e.tensor_scalar(
                    out=XBF[:, c, HALO:HALO + 1], in0=XBF[:, c, HALO:HALO + 1], scalar1=PCV[:, 0:1],
                    scalar2=None, op0=ALU.mult),
                    reads=[bXBF[c][1], bPCV], writes=[bXBF[c][1]])
            for c in cs:
                xrow = XB[:, c, :]
                P.op("act", lambda e, xrow=xrow: e.activation(out=xrow[:, 32:TT], in_=xrow[:, 32:TT], func=AF.Identity,
                                                              scale=ALPHA),
                     reads=bXB[c], writes=bXB[c])
            continue
        cur = {c: (XB[:, c, :], bXB[c]) for c in cs}
        lo = 0
        sh = 1
        if g == 3:
            for ci, c in enumerate(cs):
                xrow = XB[:, c, :]
                ea = scr[2 * ci]
                P.op("dve", lambda e, ea=ea, xrow=xrow: e.tensor_tensor(
                    out=ea[:, 16:TT], in0=xrow[:, 16:TT], in1=xrow[:, 0:TT - 16], op=ALU.subtract),
                    reads=bXB[c], writes=[bS4[2 * ci]])
            for ci, c in enumerate(cs):
                xrow = XB[:, c, :]
                ea = scr[2 * ci]
                P.op("dve", lambda e, ea=ea, xrow=xrow: e.tensor_copy(out=ea[:, 0:16], in_=xrow[:, 0:16]),
                     reads=bXB[c] + [bS4[2 * ci]], writes=[bS4[2 * ci]])
            for ci, c in enumerate(cs):
                ea = scr[2 * ci]
                sa = scr[2 * ci + 1]
                P.op("dve", lambda e, ea=ea, sa=sa: e.tensor_tensor_scan(
                    out=sa[:, 0:TT], data0=ea[:, 0:TT], data1=DUM[:, 0:1].to_broadcast([128, TT]),
                    initial=0.0, op0=ALU.add, op1=ALU.add),
                    reads=[bS4[2 * ci], bDUM], writes=[bS4[2 * ci + 1]])
                cur[c] = (sa, [bS4[2 * ci + 1]])
        for stp in range(g + 1 if g < 3 else 0):
            lo = lo + sh
            for ci, c in enumerate(cs):
                si_ = 2 * ci + (stp % 2)
                dstb = scr[si_]
                src, sb = cur[c]
                P.op("dve", lambda e, dstb=dstb, src=src, lo=lo, sh=sh: e.tensor_tensor(
                    out=dstb[:, lo:TT], in0=src[:, lo:TT], in1=src[:, lo - sh:TT - sh], op=ALU.add),
                    reads=sb, writes=[bS4[si_]])
                cur[c] = (dstb, [bS4[si_]])
            sh *= 2
        for ci, c in enumerate(cs):
            src, sb = cur[c]
            xrow = XB[:, c, :]
            P.op("dve", lambda e, src=src, c=c, w=w, xrow=xrow: e.scalar_tensor_tensor(
                out=XBF[:, c, 32:TT], in0=src[:, 32:TT], scalar=1.0 / w, in1=xrow[:, 32:TT],
                op0=ALU.mult, op1=ALU.subtract),
                reads=bXB[c] + sb, writes=bXBF[c])
        tmps = {}
        for ci, c in enumerate(cs):
            src, sb = cur[c]
            ti_ = 2 * ci + ((g + 1) % 2)
            tmpf = scr[ti_]
            tmps[c] = (tmpf, ti_)
            P.op("dve", lambda e, src=src, g=g, tmpf=tmpf: e.tensor_tensor(
                out=tmpf[:, 0:16], in0=src[:, HALO:HALO + 16], in1=PCV[:, 1 + 16 * g:1 + 16 * g + 16], op=ALU.mult),
                reads=sb + [bPCV], writes=[bS4[ti_]])
        for ci, c in enumerate(cs):
            tmpf, ti_ = tmps[c]
            xrow = XB[:, c, :]
            P.op("dve", lambda e, c=c, tmpf=tmpf, xrow=xrow: e.tensor_tensor(
                out=XBF[:, c, HALO:HALO + 16], in0=tmpf[:, 0:16], in1=xrow[:, HALO:HALO + 16], op=ALU.subtract),
                reads=[bS4[ti_]] + bXB[c], writes=[bXBF[c][0], bXBF[c][1]])
        for ci, c in enumerate(cs):
            xrow = XB[:, c, :]
            P.op("act", lambda e, xrow=xrow: e.activation(out=xrow[:, 32:TT], in_=xrow[:, 32:TT], func=AF.Identity,
                                                          scale=ALPHA),
                 reads=bXB[c], writes=bXB[c])
    fence(allHB + bS4)
    o_psh = dv_alloc("pool_scale_half", 2)
    P.op("dve", lambda e: e.tensor_scalar(out=dvcol(o_psh, 2), in0=vcol("pool_scale", 0, 2), scalar1=0.5,
                                          scalar2=None, op0=ALU.mult),
         reads=[bVEC], writes=[bDV])
    sp_ = slot_of("pool_w")
    PW = RING[:, sp_, 0:2048].rearrange("p (g k n) -> p g k n", g=4, k=2)

    def pool_mm_tile(tile):
        (ti, off, n) = tile
        for g in range(4):
            for m in range(2):
                co = 2 * g + m
                bk = next_bank()
                for k in range(2):
                    P.op("pe", lambda e, bk=bk, n=n, g=g, k=k, m=m, off=off: e.matmul(
                        PS[bk][:, 0:n], lhsT=PW[:, g, k, m * 128:(m + 1) * 128], rhs=XBF[:, 2 * g + k, off:off + n],
                        start=(k == 0), stop=(k == 1)),
                        reads=[bRING[sp_], bXBF[2 * g + k][ti]], writes=[bPS[bk]], pe_acc=(k > 0))
                dst = XB[:, co, off:off + n]
                psc = dvcol(o_psh + co) if g == 0 else vcol("pool_scale", co)
                P.op("dve", lambda e, dst=dst, bk=bk, n=n, psc=psc: e.scalar_tensor_tensor(
                    out=dst, in0=PS[bk][:, 0:n], scalar=psc, in1=dst, op0=ALU.mult, op1=ALU.add),
                    reads=[bPS[bk], bXB[co][ti], bVEC, bDV], writes=[bXB[co][ti]])
                yield

    pipeline(PIPETILES, pool_mm_tile,
             lambda t: ln_s3(t, "mix_ln_g0", "mix_ln_b0", "midx"),
             lambda t: mlp_head(0, t))
    release("pool_w")
    mlp_rest(0, ALLTILES)

    def conv_in_tile(tile):
        (ti, off, n) = tile
        for c in range(NCH):
            u, cc = c // 2, c % 2
            s = slot_of(f"cwin{u}")
            W = RING[:, s, :].rearrange("p (k n) -> p k n", k=8)
            pa = next_bank()
            pg = next_bank()
            for k in range(8):
                P.op("pe", lambda e, pa=pa, n=n, k=k, cc=cc, off=off, W=W: e.matmul(
                    PS[pa][:, 0:n], lhsT=W[:, k, cc * 128:(cc + 1) * 128], rhs=XBF[:, k, off:off + n],
                    start=(k == 0), stop=(k == 7)),
                    reads=[bRING[s], bXBF[k][ti]], writes=[bPS[pa]], pe_acc=(k > 0))
            for k in range(8):
                P.op("pe", lambda e, pg=pg, n=n, k=k, cc=cc, off=off, W=W: e.matmul(
                    PS[pg][:, 0:n], lhsT=W[:, k, 256 + cc * 128:256 + (cc + 1) * 128], rhs=XBF[:, k, off:off + n],
                    start=(k == 0), stop=(k == 7)),
                    reads=[bRING[s], bXBF[k][ti]], writes=[bPS[pg]], pe_acc=(k > 0))
            si = rot("sg")
            P.op("act", lambda e, si=si, n=n, pg=pg, c=c: e.activation(
                out=SG[:, si, 0:n], in_=PS[pg][:, 0:n], func=AF.Sigmoid, bias=vcol("conv_b_in", 8 + c)),
                reads=[bPS[pg], bVEC], writes=[bSG[si]])
            P.op("dve", lambda e, si=si, n=n, pa=pa, c=c, off=off: e.scalar_tensor_tensor(
                out=HB[:, c, off:off + n], in0=PS[pa][:, 0:n], scalar=vcol("conv_b_in", c),
                in1=SG[:, si, 0:n], op0=ALU.add, op1=ALU.mult),
                reads=[bPS[pa], bSG[si], bVEC], writes=[bHB[c][ti]])
            if ti == 0:
                P.op("dve", lambda e, c=c, off=off, n=n: e.tensor_scalar(
                    out=HB[:, c, off:off + n], in0=HB[:, c, off:off + n], scalar1=PCV[:, 0:1],
                    scalar2=None, op0=ALU.mult),
                    reads=[bHB[c][ti], bPCV], writes=[bHB[c][ti]])
            yield

    pipeline(PIPETILES, lambda t: gemm2_tile(0, 3, t),
             lambda t: ln_s3(t, "mlp_ln_g0", "mlp_ln_b0", "mid", AG["mlp0"]),
             conv_in_tile)
    mlp_tail_done(0)
    for u in range(4):
        release(f"cwin{u}")

    HC = XBF.rearrange("p c t -> p (c t)").bitcast(F32)
    bHC = [[Buf(f"HC{c}_{t}") for t in range(2)] for c in range(NCH)]
    allXBF = [b for c in range(NCH) for b in bXBF[c]]
    allHC = [b for c in range(NCH) for b in bHC[c]]
    fence(allXBF + allHC)
    sSB = [slot_of("SB0"), slot_of("SB1")]
    sDG = [slot_of("DG0"), slot_of("DG1")]
    bSB = [[Buf(f"SB{c}_{t}") for t in range(2)] for c in range(NCH)]
    bDG = [Buf("DGa"), Buf("DGb")]
    allSB = [b for c in range(NCH) for b in bSB[c]]
    fence([bRING[sSB[0]], bRING[sSB[1]], bRING[sDG[0]], bRING[sDG[1]]] + allSB + bDG)
    SBv = [RING[:, sSB[i], :].rearrange("p (c t n) -> p c t n", c=4, t=2) for i in range(2)]
    DGv = [RING[:, sDG[i], 0:KW * 128].rearrange("p (k n) -> p k n", k=KW) for i in range(2)]
    so0 = slot_of("cwout0")
    so1 = slot_of("cwout1")
    Wo = [RING[:, so0, :].rearrange("p (k n) -> p k n", k=8), RING[:, so1, :].rearrange("p (k n) -> p k n", k=8)]
    bWo = [bRING[so0], bRING[so1]]
    dwo, _ = VEC_LAYOUT["conv_dw"]
    ido, _ = VEC_LAYOUT["ident"]
    dgi = [0]

    def hc_ap(c, tl):
        o = (c * 2 + tl) * 512
        return HC[:, o:o + 512]

    def w_out_tile(tile, tl):
        (ti, off, n) = tile
        for m in range(NCH):
            bk = next_bank()
            for k in range(8):
                P.op("pe", lambda e, bk=bk, k=k, m=m, tl=tl: e.matmul(
                    PS[bk][:, 0:512], lhsT=Wo[m // 4][:, k, (m % 4) * 128:(m % 4 + 1) * 128],
                    rhs=SBv[k // 4][:, k % 4, tl, :], start=(k == 0), stop=(k == 7)),
                    reads=[bWo[m // 4], bSB[k][tl]], writes=[bPS[bk]], pe_acc=(k > 0))
            dst = XB[:, m, off:off + 512]
            P.op("dve", lambda e, dst=dst, bk=bk: e.tensor_tensor(out=dst, in0=PS[bk][:, 0:512], in1=dst, op=ALU.add),
                 reads=[bPS[bk], bXB[m][ti]], writes=[bXB[m][ti]])
            yield

    dg_seq = [(t, c) for t in range(4) for c in range(NCH)]

    def build_dg(i):
        c = dg_seq[i][1]
        di = i % 2
        P.op("dve", lambda e, di=di, c=c: e.tensor_tensor(
            out=DGv[di], in0=VEC[:, ido:ido + 128].unsqueeze(1).to_broadcast([128, KW, 128]),
            in1=VEC[:, dwo + c * KW:dwo + (c + 1) * KW].unsqueeze(2).to_broadcast([128, KW, 128]),
            op=ALU.mult),
            reads=[bVEC], writes=[bDG[di]])

    def conv_ln_chain(par):
        ln_apply_stats(par, 4 + 2 * par, 5 + 2 * par, 512)

    def conv_ln_apply(par):
        for c in range(NCH):
            hca = hc_ap(c, par)
            P.op("dve", lambda e, hca=hca, par=par: e.tensor_tensor(out=hca, in0=hca, in1=LNB[:, par, :], op=ALU.mult),
                 reads=[bHC[c][par], bLNB[par]], writes=[bHC[c][par]])
        for c in range(NCH):
            hca = hc_ap(c, par)
            P.op("dve", lambda e, hca=hca, par=par: e.tensor_tensor(out=hca, in0=hca, in1=LNA[:, par, :], op=ALU.add),
                 reads=[bHC[c][par], bLNA[par]], writes=[bHC[c][par]])

    def conv_silu(par):
        for c in range(NCH):
            hca = hc_ap(c, par)
            P.op("act", lambda e, hca=hca, c=c, par=par: e.activation(
                out=SBv[c // 4][:, c % 4, par, :], in_=hca, func=AF.Silu,
                bias=vcol("conv_ln_b", c), scale=vcol("conv_ln_g", c)),
                reads=[bHC[c][par], bVEC], writes=[bSB[c][par]])

    def conv_side(tprev):
        for _ in conv_side_ln(tprev):
            yield
        for _ in w_out_tile(MTILES[tprev], tprev % 2):
            yield

    def conv_side_ln(tprev):
        pp = tprev % 2
        bmp, bep = 4 + 2 * pp, 5 + 2 * pp
        bb = LNB[:, pp, :]
        P.op("act", lambda e: e.activation(out=bb, in_=PS[bmp][:, 0:512], func=AF.Square),
             reads=[bPS[bmp]], writes=[bLNB[pp]])
        P.op("dve", lambda e: e.scalar_tensor_tensor(out=bb, in0=PS[bep][:, 0:512], scalar=EPS, in1=bb,
                                                     op0=ALU.add, op1=ALU.subtract),
             reads=[bPS[bep], bLNB[pp]], writes=[bLNB[pp]])
        P.op("act", lambda e: e.activation(out=bb, in_=bb, func=AF.Ln),
             reads=[bLNB[pp]], writes=[bLNB[pp]])
        P.op("act", lambda e: e.activation(out=bb, in_=bb, func=AF.Exp, scale=-0.5),
             reads=[bLNB[pp]], writes=[bLNB[pp]])
        yield
        for c in range(NCH):
            hca = hc_ap(c, pp)
            P.op("dve", lambda e, hca=hca, bmp=bmp: e.tensor_tensor(out=hca, in0=hca, in1=PS[bmp][:, 0:512], op=ALU.subtract),
                 reads=[bHC[c][pp], bPS[bmp]], writes=[bHC[c][pp]])
            if c % 2 == 1:
                yield
        for c in range(NCH):
            hca = hc_ap(c, pp)
            P.op("dve", lambda e, hca=hca, pp=pp: e.tensor_tensor(out=hca, in0=hca, in1=LNB[:, pp, :], op=ALU.mult),
                 reads=[bHC[c][pp], bLNB[pp]], writes=[bHC[c][pp]])
            P.op("act", lambda e, hca=hca, c=c, pp=pp: e.activation(
                out=SBv[c // 4][:, c % 4, pp, :], in_=hca, func=AF.Silu,
                bias=vcol("conv_ln_b", c), scale=vcol("conv_ln_g", c)),
                reads=[bHC[c][pp], bVEC], writes=[bSB[c][pp]])
            if c % 2 == 1:
                yield

    def advance(gen, k):
        if gen is None:
            return
        for _ in range(k):
            try:
                next(gen)
            except StopIteration:
                return

    gemm_banks[0] = [0, 1, 2, 3]
    gemm_bank[0] = 0
    st_ctr = [0]
    build_dg(0)
    for t in range(4):
        (ti, off, n) = MTILES[t]
        par = t % 2
        side = conv_side(t - 1) if t > 0 else None
        deferred = []
        for c in range(NCH):
            i = t * NCH + c
            di = i % 2
            if i + 1 < len(dg_seq):
                build_dg(i + 1)
            bk = next_bank()
            for k in range(KW):
                P.op("pe", lambda e, bk=bk, di=di, k=k, c=c, off=off: e.matmul(
                    PS[bk][:, 0:512], lhsT=DGv[di][:, k, :], rhs=HB[:, c, off - 30 + k:off - 30 + k + 512],
                    start=(k == 0), stop=(k == KW - 1)),
                    reads=[bDG[di], bHB[c][ti], bHB[c][ti - 1]], writes=[bPS[bk]], pe_acc=(k > 0))
            while len(deferred) >= 2:
                deferred.pop(0)()
            hca = hc_ap(c, par)
            P.op("act", lambda e, hca=hca, bk=bk, c=c: e.activation(
                out=hca, in_=PS[bk][:, 0:512], func=AF.Identity, bias=vcol("conv_dw_b", c)),
                reads=[bPS[bk], bVEC], writes=[bHC[c][par]])
            st_ctr[0] = (st_ctr[0] + 1) % 3
            si = st_ctr[0]
            P.op("act", lambda e, si=si, bk=bk, c=c: e.activation(
                out=VSQ[:, si, :], in_=PS[bk][:, 0:512], func=AF.Square, bias=vcol("conv_dw_b", c)),
                reads=[bPS[bk], bVEC], writes=[bVSQ[si]])
            vi = si
            P.op("act", lambda e, vi=vi, bk=bk, c=c: e.activation(
                out=VBT[:, vi, :], in_=PS[bk][:, 0:512], func=AF.Identity, bias=vcol("conv_dw_b", c)),
                reads=[bPS[bk], bVEC], writes=[bVBT[vi]])
            bm, be = 4 + 2 * par, 5 + 2 * par

            def stats(vi=vi, si=si, bm=bm, be=be, c=c):
                P.op("pe", lambda e: e.matmul(PS[bm][:, 0:512], lhsT=ONES, rhs=VBT[:, vi, :],
                                              start=(c == 0), stop=(c == NCH - 1)),
                     reads=[bONES, bVBT[vi]], writes=[bPS[bm]], pe_acc=(c > 0))
                P.op("pe", lambda e: e.matmul(PS[be][:, 0:512], lhsT=ONES, rhs=VSQ[:, si, :],
                                              start=(c == 0), stop=(c == NCH - 1)),
                     reads=[bONES, bVSQ[si]], writes=[bPS[be]], pe_acc=(c > 0))
            deferred.append(stats)
            advance(side, 2 if c < 4 else (3 if c < 7 else 99))
        for f in deferred:
            f()
    fence([bRING[sDG[0]], bRING[sDG[1]]] + bDG)
    release("DG0")
    release("DG1")
    run(conv_side_ln(3))
    gemm_banks[0] = [0, 1, 2, 3, 6, 7]
    fence(allXBF + allHC)

    def x3(tile):
        if tile[0] == 4:
            return w_out_tile(tile, 1)
        return None

    pipeline(MTILES, x3,
             lambda t: ln_s3(t, "mix_ln_g1", "mix_ln_b1", "midx"),
             lambda t: mlp_head(1, t))
    fence([bRING[sSB[0]], bRING[sSB[1]]] + allSB)
    release("cwout0")
    release("cwout1")
    release("SB0")
    release("SB1")
    mlp_rest(1, MTILES)
    pipeline(MTILES, lambda t: gemm2_tile(1, 3, t),
             lambda t: ln_s3(t, "mlp_ln_g1", "mlp_ln_b1", "final"), None)
    mlp_tail_done(1)
    return finish()


def _emit(nc, P):
    P.resolve()
    from contextlib import ExitStack
    with ExitStack() as es:
        sems = {}
        for name in P.sem_names:
            sems[name] = es.enter_context(nc.semaphore(name))
        block = es.enter_context(nc.Block())
        outsem_total = P.final.get("out", 0)

        @block.tensor
        def _(eng):
            P.emit_engine("pe", eng, sems)

        @block.scalar
        def _(eng):
            P.emit_engine("act", eng, sems)

        @block.vector
        def _(eng):
            P.emit_engine("dve", eng, sems)

        @block.gpsimd
        def _(eng):
            P.emit_engine("pool", eng, sems)
            for name in P.sem_names:
                if name.startswith("ring"):
                    eng.wait_ge(sems[name], P.final[name])

        @block.sync
        def _(eng):
            P.emit_engine("sp", eng, sems)
            for name in P.sem_names:
                if name in ("vec", "pcv") or name.startswith("xin"):
                    eng.wait_ge(sems[name], P.final[name])
            eng.wait_ge(sems["out"], outsem_total)
    return nc


def _fm(v, n):
    return np.ascontiguousarray(np.asarray(v, np.float32).reshape(n, 128).T)


def _pack_vecs(inp):
    V = np.zeros((128, NV), np.float32)

    def put(name, arr):
        o, n = VEC_LAYOUT[name]
        assert arr.shape == (128, n), (name, arr.shape, n)
        V[:, o:o + n] = arr

    put("pool_scale", _fm(inp["pool_scale"][0], 8))
    for l in range(2):
        put(f"mix_ln_g{l}", _fm(inp["mix_ln_g"][l], 8))
        put(f"mix_ln_b{l}", _fm(inp["mix_ln_b"][l], 8))
        put(f"mlp_b1_{l}", _fm(inp["mlp_b1"][l], 32))
        put(f"mlp_b2_{l}", _fm(inp["mlp_b2"][l], 8))
        put(f"mlp_ln_g{l}", _fm(inp["mlp_ln_g"][l], 8))
        put(f"mlp_ln_b{l}", _fm(inp["mlp_ln_b"][l], 8))
    put("conv_b_in", _fm(inp["conv_b_in"][0], 16))
    dw = np.asarray(inp["conv_dw"][0], np.float32)
    dwl = dw.reshape(KW, 8, 128).transpose(2, 1, 0).reshape(128, 8 * KW)
    put("conv_dw", np.ascontiguousarray(dwl))
    put("conv_dw_b", _fm(inp["conv_dw_b"][0], 8))
    put("conv_ln_g", _fm(inp["conv_ln_g"][0], 8))
    put("conv_ln_b", _fm(inp["conv_ln_b"][0], 8))
    put("conv_b_out", _fm(inp["conv_b_out"][0], 8))
    put("ident", np.eye(128, dtype=np.float32))
    return V


def _percore_tables():
    tabs = []
    for core in range(NCORES):
        first = (core % 4 == 0)
        t = np.zeros((128, NPC), np.float32)
        t[:, 0] = 0.0 if first else 1.0
        for g, w in enumerate(POOLW):
            for i in range(16):
                cntv = min(i + 1, w) if first else w
                t[:, 1 + 16 * g + i] = np.float32(1.0) / np.float32(cntv)
        tabs.append(t)
    return tabs


_CACHE = {}


def _get_nc(dbg_stage=None):
    if dbg_stage not in _CACHE:
        nc, P = build_program(dbg_stage)
        _CACHE[dbg_stage] = _emit(nc, P)
    return _CACHE[dbg_stage]


def _in_maps(inp):
    x = np.asarray(inp["x"], np.float32)
    V = _pack_vecs(inp)
    tabs = _percore_tables()
    shared = {
        "vecs": V,
        "pool_w": np.ascontiguousarray(np.asarray(inp["pool_w"], np.float32).reshape(1024, 256)),
        "conv_w_in": np.ascontiguousarray(np.asarray(inp["conv_w_in"], np.float32).reshape(1024, 2048)),
        "conv_w_out": np.ascontiguousarray(np.asarray(inp["conv_w_out"], np.float32).reshape(1024, 1024)),
        "mlp_w1": np.ascontiguousarray(np.asarray(inp["mlp_w1"], np.float32).reshape(2048, DFF)),
        "mlp_w2": np.ascontiguousarray(np.asarray(inp["mlp_w2"], np.float32).reshape(2 * DFF, 1024)),
    }
    maps = []
    for core in range(NCORES):
        b, ch = core // 4, core % 4
        t0 = ch * TPC
        xs = np.zeros((TT, D), np.float32)
        lo = max(0, t0 - HALO)
        xs[HALO - (t0 - lo):, :] = x[b, lo:t0 + TPC, :]
        m = dict(shared)
        m["xT"] = np.ascontiguousarray(xs.T)
        m["pcore"] = tabs[core]
        maps.append(m)
    return maps


def kernel(**inputs):
    nc = _get_nc(None)
    maps = _in_maps(inputs)
    res = run_bass_kernel_spmd(nc, maps, core_ids=list(range(NCORES)))
    out = np.empty((BATCH, SEQ, D), np.float32)
    for core in range(NCORES):
        b, ch = core // 4, core % 4
        y = np.asarray(res.results[core]["yT"]).reshape(4, NCH, 128, 512)
        out[b, ch * TPC:(ch + 1) * TPC, :] = y.transpose(0, 3, 1, 2).reshape(TPC, D)
    return out
```
